# Optimizing a Trainium2 kernel written in Bass

```python
import math
import jax, jax.numpy as jnp
from jax import lax
import numpy as np

D_MODEL = 1024
BATCH = 16
SEQ = 2048
DEPTH = 2

PLE_DIM = 256
ROPE_THETA = 500000.0
MAX_POS_OFFSET = 4096
A_HEAD_DIM = 64
A_HEADS = D_MODEL // (2 * A_HEAD_DIM)
A_ROT = A_HEAD_DIM // 4
A_PATTERNS = ((128, 1), (512, 4), (2048, 16))
B_HEADS = D_MODEL // 128
B_Q_LORA = 3 * D_MODEL // 8
B_KV_LORA = D_MODEL // 4
B_NOPE = 64
B_ROPE = 32
B_V = 64
C_HEAD_DIM = 64
C_HEADS = D_MODEL // (2 * C_HEAD_DIM)
C_ROT = C_HEAD_DIM // 4
D_FF = 256 * math.ceil(8 * D_MODEL / 3 / 256)
CONV_WIDTH = 3
Q_BLOCK = 128
LN_EPS = 1e-5
RMS_EPS = 1e-6
NEG_INF = -1e30
N_AB = (DEPTH + 1) // 2
N_C = DEPTH // 2
ALPHA = (2 * DEPTH) ** 0.25
BETA = (8 * DEPTH) ** -0.25
A_WIDTH = A_HEADS * A_HEAD_DIM
AB_IN = 3 * A_WIDTH + B_Q_LORA + B_KV_LORA + B_ROPE
AB_MIX = A_WIDTH + B_HEADS * B_V
C_QK = C_HEADS * 2 * C_HEAD_DIM
C_MIX = C_HEADS * 2 * C_HEAD_DIM

kernel_name = "hybrid_dilated_mla_diff_encoder"


def layer_norm(x, g, b):
    xf = x.astype(jnp.float32)
    mu = jnp.mean(xf, axis=-1, keepdims=True)
    var = jnp.mean(jnp.square(xf - mu), axis=-1, keepdims=True)
    y = (xf - mu) * lax.rsqrt(var + LN_EPS)
    return (y * g.astype(jnp.float32) + b.astype(jnp.float32)).astype(x.dtype)


def rms_norm(x, g, eps):
    xf = x.astype(jnp.float32)
    y = xf * lax.rsqrt(jnp.mean(jnp.square(xf), axis=-1, keepdims=True) + eps)
    return (y * g.astype(jnp.float32)).astype(x.dtype)


def rope_tables(positions, rot):
    inv_freq = 1.0 / (ROPE_THETA ** (jnp.arange(0, rot, 2, dtype=jnp.float32) / rot))
    ang = positions.astype(jnp.float32)[..., None] * inv_freq
    return jnp.cos(ang), jnp.sin(ang)


def apply_rope(x, cos, sin, rot):
    bshape = cos.shape[:2] + (1,) * (x.ndim - 3) + cos.shape[-1:]
    c = cos.reshape(bshape).astype(x.dtype)
    s = sin.reshape(bshape).astype(x.dtype)
    half = rot // 2
    x1, x2, rest = x[..., :half], x[..., half:rot], x[..., rot:]
    return jnp.concatenate([x1 * c - x2 * s, x2 * c + x1 * s, rest], axis=-1)


def dilated_window_attention(q, k, v, window, dilation):
    B, H, S, dh = q.shape
    n_side = window // (2 * dilation)
    L = S // dilation
    blk = n_side
    nb = -(-L // blk)
    Lp = nb * blk

    def strided(t):
        return t.reshape(B, H, L, dilation, dh).transpose(0, 1, 3, 2, 4)

    def slabs(t):
        tp = jnp.pad(t, ((0, 0), (0, 0), (0, 0), (blk, Lp - L + blk), (0, 0)))
        tp = tp.reshape(B, H, dilation, nb + 2, blk, dh)
        return jnp.concatenate([tp[:, :, :, 0:nb], tp[:, :, :, 1:nb + 1], tp[:, :, :, 2:nb + 2]], axis=4)

    qb = jnp.pad(strided(q), ((0, 0), (0, 0), (0, 0), (0, Lp - L), (0, 0))).reshape(B, H, dilation, nb, blk, dh)
    kb = slabs(strided(k))
    vb = slabs(strided(v))
    s = jnp.einsum("bhrnqd,bhrnkd->bhrnqk", qb, kb) * (dh ** -0.5)
    qi = jnp.arange(nb)[:, None, None] * blk + jnp.arange(blk)[None, :, None]
    ki = jnp.arange(nb)[:, None, None] * blk - blk + jnp.arange(3 * blk)[None, None, :]
    valid = (ki >= 0) & (ki < L) & (jnp.abs(ki - qi) <= n_side)
    s = jnp.where(valid, s, NEG_INF)
    m = jnp.max(s, axis=-1, keepdims=True)
    e = jnp.exp(s - m)
    l = jnp.sum(e, axis=-1)
    o = jnp.einsum("bhrnqk,bhrnkd->bhrnqd", e, vb) / l[..., None]
    lse = m[..., 0] + jnp.log(l)
    o = o.reshape(B, H, dilation, Lp, dh)[:, :, :, :L].transpose(0, 1, 3, 2, 4).reshape(B, H, S, dh)
    lse = lse.reshape(B, H, dilation, Lp)[..., :L].transpose(0, 1, 3, 2).reshape(B, H, S)
    return o, lse


def dilated_mixture(q, k, v):
    B, S, H, dh = q.shape
    qf, kf, vf = (t.astype(jnp.float32).transpose(0, 2, 1, 3) for t in (q, k, v))
    outs, lses = [], []
    for window, dilation in A_PATTERNS:
        o, lse = dilated_window_attention(qf, kf, vf, window, dilation)
        outs.append(o)
        lses.append(lse)
    w = jax.nn.softmax(jnp.stack(lses), axis=0)
    o = jnp.sum(w[..., None] * jnp.stack(outs), axis=0)
    return o.transpose(0, 2, 1, 3).reshape(B, S, H * dh)


def dense_attention(q, k, v, scale):
    B, S, H, dq = q.shape
    dv = v.shape[-1]
    nq = S // Q_BLOCK
    kf = k.astype(jnp.float32).transpose(0, 2, 1, 3)
    vf = v.astype(jnp.float32).transpose(0, 2, 1, 3)
    qb = q.astype(jnp.float32).transpose(0, 2, 1, 3).reshape(B, H, nq, Q_BLOCK, dq).transpose(2, 0, 1, 3, 4)

    def block(qi):
        s = jnp.einsum("bhqd,bhkd->bhqk", qi, kf) * scale
        return jnp.einsum("bhqk,bhkd->bhqd", jax.nn.softmax(s, axis=-1), vf)

    o = lax.map(block, qb)
    return o.transpose(1, 0, 3, 2, 4).reshape(B, S, H * dv)


def diff_attention(q, k, v, lam):
    B, S, H, _, dh = q.shape
    dv = v.shape[-1]
    nq = S // Q_BLOCK
    kf = k.astype(jnp.float32).transpose(0, 2, 3, 1, 4)
    vf = v.astype(jnp.float32).transpose(0, 2, 1, 3)
    qb = q.astype(jnp.float32).transpose(0, 2, 3, 1, 4).reshape(B, H, 2, nq, Q_BLOCK, dh).transpose(3, 0, 1, 2, 4, 5)

    def block(qi):
        s = jnp.einsum("bhcqd,bhckd->bhcqk", qi, kf) * (dh ** -0.5)
        a = jax.nn.softmax(s, axis=-1)
        return jnp.einsum("bhqk,bhkd->bhqd", a[:, :, 0] - lam * a[:, :, 1], vf)

    o = lax.map(block, qb)
    return o.transpose(1, 0, 3, 2, 4).reshape(B, S, H, dv)


def mixer_ab(x, cos_a, sin_a, cos_b, sin_b, w_in, q_norm, w_q_up, kv_norm, w_kv_up, w_out):
    B, S, _ = x.shape
    h = x @ w_in
    o1 = A_WIDTH
    o2 = 2 * A_WIDTH
    o3 = 3 * A_WIDTH
    o4 = o3 + B_Q_LORA
    o5 = o4 + B_KV_LORA
    shape_a = (B, S, A_HEADS, A_HEAD_DIM)
    qa = apply_rope(h[..., :o1].reshape(shape_a), cos_a, sin_a, A_ROT)
    ka = apply_rope(h[..., o1:o2].reshape(shape_a), cos_a, sin_a, A_ROT)
    va = h[..., o2:o3].reshape(shape_a)
    out_a = dilated_mixture(qa, ka, va).astype(x.dtype)
    cq = rms_norm(h[..., o3:o4], q_norm, RMS_EPS)
    qb = (cq @ w_q_up).reshape(B, S, B_HEADS, B_NOPE + B_ROPE)
    q_pe = apply_rope(qb[..., B_NOPE:], cos_b, sin_b, B_ROPE)
    qb = jnp.concatenate([qb[..., :B_NOPE], q_pe], axis=-1)
    ckv = rms_norm(h[..., o4:o5], kv_norm, RMS_EPS)
    kv = (ckv @ w_kv_up).reshape(B, S, B_HEADS, B_NOPE + B_V)
    k_pe = apply_rope(h[..., o5:].reshape(B, S, 1, B_ROPE), cos_b, sin_b, B_ROPE)
    kb = jnp.concatenate([kv[..., :B_NOPE], jnp.broadcast_to(k_pe, (B, S, B_HEADS, B_ROPE))], axis=-1)
    vb = kv[..., B_NOPE:]
    out_b = dense_attention(qb, kb, vb, (B_NOPE + B_ROPE) ** -0.5).astype(x.dtype)
    return jnp.concatenate([out_a, out_b], axis=-1) @ w_out


def mixer_c(x, cos_c, sin_c, w_qkv, lam_params, subln, w_out, lambda_init):
    B, S, _ = x.shape
    h = x @ w_qkv
    q = apply_rope(h[..., :C_QK].reshape(B, S, C_HEADS, 2, C_HEAD_DIM), cos_c, sin_c, C_ROT)
    k = apply_rope(h[..., C_QK:2 * C_QK].reshape(B, S, C_HEADS, 2, C_HEAD_DIM), cos_c, sin_c, C_ROT)
    v = h[..., 2 * C_QK:].reshape(B, S, C_HEADS, 2 * C_HEAD_DIM)
    lp = lam_params.astype(jnp.float32)
    lam = jnp.exp(jnp.sum(lp[0] * lp[1])) - jnp.exp(jnp.sum(lp[2] * lp[3])) + lambda_init
    o = diff_attention(q, k, v, lam)
    o = rms_norm(o, subln, LN_EPS) * (1.0 - lambda_init)
    return o.reshape(B, S, C_MIX).astype(x.dtype) @ w_out


def conv_ffn(x, w_gate, w_up, conv_w, conv_b, w_down):
    S = x.shape[1]
    a = x @ w_gate
    u = x @ w_up
    pad = CONV_WIDTH // 2
    ap = jnp.pad(a, ((0, 0), (pad, pad), (0, 0)))
    c = conv_b
    for j in range(CONV_WIDTH):
        c = c + conv_w[j] * ap[:, j:j + S]
    return (jax.nn.gelu(c) * u) @ w_down


def setup_inputs(seed: int = 0) -> dict:
    key = jax.random.key(seed)
    ks = jax.random.split(key, 24)
    f32 = jnp.float32

    def w(k, shape, fan_in, gain=1.0):
        return jax.random.normal(k, shape, f32) * (gain * fan_in ** -0.5)

    def gain_init(k, shape):
        return 1.0 + 0.02 * jax.random.normal(k, shape, f32)

    def bias_init(k, shape):
        return 0.02 * jax.random.normal(k, shape, f32)

    x = jax.random.normal(ks[0], (BATCH, SEQ, D_MODEL), f32)
    p = jax.random.normal(ks[1], (DEPTH, BATCH, SEQ, PLE_DIM), f32)
    positions = (jnp.arange(SEQ, dtype=jnp.int32)[None, :]
                 + jax.random.randint(ks[2], (BATCH, 1), 0, MAX_POS_OFFSET, dtype=jnp.int32))
    return {
        "x": x,
        "p": p,
        "positions": positions,
        "ab_w_in": w(ks[3], (N_AB, D_MODEL, AB_IN), D_MODEL),
        "ab_q_norm": gain_init(ks[4], (N_AB, B_Q_LORA)),
        "ab_w_q_up": w(ks[5], (N_AB, B_Q_LORA, B_HEADS * (B_NOPE + B_ROPE)), B_Q_LORA),
        "ab_kv_norm": gain_init(ks[6], (N_AB, B_KV_LORA)),
        "ab_w_kv_up": w(ks[7], (N_AB, B_KV_LORA, B_HEADS * (B_NOPE + B_V)), B_KV_LORA),
        "ab_w_out": w(ks[8], (N_AB, AB_MIX, D_MODEL), AB_MIX, BETA),
        "c_w_qkv": w(ks[9], (N_C, D_MODEL, 2 * C_QK + C_MIX), D_MODEL),
        "c_lambda": 0.1 * jax.random.normal(ks[10], (N_C, 4, C_HEAD_DIM), f32),
        "c_subln": gain_init(ks[11], (N_C, 2 * C_HEAD_DIM)),
        "c_w_out": w(ks[12], (N_C, C_MIX, D_MODEL), C_MIX, BETA),
        "ln_mix_g": gain_init(ks[13], (DEPTH, D_MODEL)),
        "ln_mix_b": bias_init(ks[14], (DEPTH, D_MODEL)),
        "ffn_w_gate": w(ks[15], (DEPTH, D_MODEL, D_FF), D_MODEL),
        "ffn_w_up": w(ks[16], (DEPTH, D_MODEL, D_FF), D_MODEL),
        "ffn_conv_w": w(ks[17], (DEPTH, CONV_WIDTH, D_FF), CONV_WIDTH),
        "ffn_conv_b": bias_init(ks[18], (DEPTH, D_FF)),
        "ffn_w_down": w(ks[19], (DEPTH, D_FF, D_MODEL), D_FF, BETA),
        "ln_ffn_g": gain_init(ks[20], (DEPTH, D_MODEL)),
        "ln_ffn_b": bias_init(ks[21], (DEPTH, D_MODEL)),
        "ple_w_gate": w(ks[22], (DEPTH, D_MODEL, D_MODEL), D_MODEL),
        "ple_w_proj": w(ks[23], (DEPTH, PLE_DIM, D_MODEL), PLE_DIM),
    }


def reference(x, p, positions, ab_w_in, ab_q_norm, ab_w_q_up, ab_kv_norm, ab_w_kv_up, ab_w_out,
              c_w_qkv, c_lambda, c_subln, c_w_out, ln_mix_g, ln_mix_b, ffn_w_gate, ffn_w_up,
              ffn_conv_w, ffn_conv_b, ffn_w_down, ln_ffn_g, ln_ffn_b, ple_w_gate, ple_w_proj):
    cos_a, sin_a = rope_tables(positions, A_ROT)
    cos_b, sin_b = rope_tables(positions, B_ROPE)
    cos_c, sin_c = rope_tables(positions, C_ROT)
    for i in range(DEPTH):
        j = i // 2
        if i % 2 == 0:
            y = mixer_ab(x, cos_a, sin_a, cos_b, sin_b, ab_w_in[j], ab_q_norm[j], ab_w_q_up[j],
                         ab_kv_norm[j], ab_w_kv_up[j], ab_w_out[j])
        else:
            lambda_init = 0.8 - 0.6 * math.exp(-0.3 * i)
            y = mixer_c(x, cos_c, sin_c, c_w_qkv[j], c_lambda[j], c_subln[j], c_w_out[j], lambda_init)
        x = layer_norm(ALPHA * x + y, ln_mix_g[i], ln_mix_b[i])
        f = conv_ffn(x, ffn_w_gate[i], ffn_w_up[i], ffn_conv_w[i], ffn_conv_b[i], ffn_w_down[i])
        x = layer_norm(ALPHA * x + f, ln_ffn_g[i], ln_ffn_b[i])
        x = x + jax.nn.sigmoid(x @ ple_w_gate[i]) * (p[i] @ ple_w_proj[i])
    return x
```

```python
import math
from contextlib import ExitStack

import numpy as np
import concourse.bass as bass
import concourse.mybir as mybir
from concourse.bass_utils import run_bass_kernel_spmd

F32 = mybir.dt.float32
BF16 = mybir.dt.bfloat16
I32 = mybir.dt.int32
AF = mybir.ActivationFunctionType
ALU = mybir.AluOpType

D = 1024
T = 2048
NSEQ = 2
NT = NSEQ * T
DFF = 2816
NFC = DFF // 128
DEPTH = 2
ALPHA = float((2 * DEPTH) ** 0.25)
LN_EPS = 1e-5
RMS_EPS = 1e-6
ROPE_THETA = 500000.0
TWO_PI = float(2 * np.pi)
MASK_U0 = 1920
MASK_W = 3968


class Buf:
    def __init__(self, t=None, name=""):
        self.t = t
        self.name = name
        self.w = {}
        self.r = {}
        self.g = {}

    def __getitem__(self, idx):
        return self.t[idx]


class Ring:
    def __init__(self, bufs):
        self.bufs = bufs
        self.i = 0

    def next(self):
        b = self.bufs[self.i]
        self.i = (self.i + 1) % len(self.bufs)
        return b


class Sched:
    ENGS = ("pe", "act", "dve", "pool", "sp")

    def __init__(self, nc, n_dma_sems=14):
        self.nc = nc
        self.ops = {e: [] for e in self.ENGS}
        self.cnt = {e: 0 for e in self.ENGS}
        self.waited = {e: {} for e in self.ENGS}
        self.n_dma = n_dma_sems
        self.dma_cnt = {}
        self.dma_rr = {"sp": 0, "pool": 0, "act": 0}
        self.sems = {}
        self.out_evs = []

    def op(self, eng, fn, reads=(), writes=(), accum=(), dma=False):
        waits = {}
        own = "e" + eng

        def need(k, v, raw):
            if k == own and eng == "pe":
                return
            if v > waits.get(k, 0):
                waits[k] = v

        for b in reads:
            for k, v in b.w.items():
                need(k, v, True)
        for b in writes:
            for k, v in b.w.items():
                need(k, v, False)
            for k, v in b.r.items():
                need(k, v, False)
        for b in accum:
            for k, v in b.r.items():
                need(k, v, False)
            if not b.r:
                for k, v in b.g.items():
                    need(k, v, False)
        if dma:
            i = self.dma_rr[eng]
            self.dma_rr[eng] = (i + 1) % self.n_dma
            key = "d%s%d" % (eng, i)
            prev = self.dma_cnt.get(key, 0)
            if prev > 0 and prev > waits.get(key, 0):
                waits[key] = prev
            val = prev + 16
            self.dma_cnt[key] = val
        else:
            self.cnt[eng] += 1
            key, val = own, self.cnt[eng]
        wl = []
        wd = self.waited[eng]
        for k, v in waits.items():
            if wd.get(k, 0) >= v:
                continue
            wd[k] = v
            wl.append((k, v))
        for b in reads:
            if b.r.get(key, 0) < val:
                b.r[key] = val
        for b in writes:
            g = dict(b.w)
            for k, v in b.r.items():
                if g.get(k, 0) < v:
                    g[k] = v
            b.g = g
            b.w = {key: val}
            b.r = {}
        for b in accum:
            if b.r:
                b.g = dict(b.r)
                b.w = {key: val}
                b.r = {}
            elif b.w.get(key, 0) < val:
                b.w[key] = val
        self.ops[eng].append((wl, fn, key))
        return (key, val)

    def emit(self, es):
        nc = self.nc
        self.ops["sp"].append((list(self.out_evs), None, None))
        keys = set()
        for e in self.ENGS:
            for wl, fn, key in self.ops[e]:
                if key is not None:
                    keys.add(key)
                for k, v in wl:
                    keys.add(k)
        for k in sorted(keys):
            self.sems[k] = es.enter_context(nc.semaphore(k))
        block = es.enter_context(nc.Block())
        sems = self.sems

        def run(engname):
            def body(eng):
                for wl, fn, key in self.ops[engname]:
                    for k, v in wl:
                        eng.wait_ge(sems[k], v)
                    if fn is None:
                        continue
                    ins = fn(eng)
                    ins.then_inc(sems[key], 16 if key[0] == "d" else 1)
            return body

        block.tensor(run("pe"))
        block.scalar(run("act"))
        block.vector(run("dve"))
        block.gpsimd(run("pool"))
        block.sync(run("sp"))


def _pair_partner_cols(base, nheads, hd, rot):
    half = rot // 2
    idx = []
    for h in range(nheads):
        b = base + h * hd
        idx += [b + half + i for i in range(half)] + [b + i for i in range(half)] + [b + i for i in range(rot, hd)]
    return np.array(idx)


def _interleave_pairs(W, base, npairs):
    cols = []
    for j in range(npairs):
        main = np.arange(base + j * 128, base + (j + 1) * 128)
        part = _pair_partner_cols(base + j * 128, 2, 64, 16)
        cols += [main, part]
    return np.concatenate(cols)


def pack_weights(inp):
    w = {}
    w_in = inp["ab_w_in"][0]
    cq0, ckv0, kpe0 = 1536, 1920, 2176
    kpe = np.arange(kpe0, kpe0 + 32)
    kpe_p = np.concatenate([kpe[16:], kpe[:16]])
    dummy = np.arange(0, 64)
    cols0 = np.concatenate([
        np.arange(0, 1024),
        np.arange(cq0, cq0 + 384), np.arange(ckv0, ckv0 + 256),
        dummy, kpe, dummy, kpe_p])
    w["w0p"] = np.ascontiguousarray(w_in[:, cols0])
    w["w0v"] = np.ascontiguousarray(w_in[:, 1024:1536])
    wq = inp["ab_w_q_up"][0]
    qcols = []
    for h in range(8):
        b = h * 96
        main = np.arange(b, b + 96)
        part = np.concatenate([np.arange(b, b + 64), np.arange(b + 80, b + 96), np.arange(b + 64, b + 80)])
        qcols += [main, part]
    w["wqu"] = np.ascontiguousarray(wq[:, np.concatenate(qcols)])
    wkv = inp["ab_w_kv_up"][0]
    kn = np.concatenate([np.arange(h * 128, h * 128 + 64) for h in range(8)])
    vb = np.concatenate([np.arange(h * 128 + 64, h * 128 + 128) for h in range(8)])
    w["wkn"] = np.ascontiguousarray(wkv[:, kn])
    w["wvb"] = np.ascontiguousarray(wkv[:, vb])
    w["wo0"] = np.ascontiguousarray(inp["ab_w_out"][0])
    wc = inp["c_w_qkv"][0]
    w["w1p"] = np.ascontiguousarray(wc[:, 0:2048])
    w["w1v"] = np.ascontiguousarray(wc[:, 2048:3072])
    w["wo1"] = np.ascontiguousarray(inp["c_w_out"][0])
    for l in range(DEPTH):
        w["wg%d" % l] = np.ascontiguousarray(inp["ffn_w_gate"][l])
        w["wu%d" % l] = np.ascontiguousarray(inp["ffn_w_up"][l])
        w["wd%d" % l] = np.ascontiguousarray(inp["ffn_w_down"][l])
        w["wpg%d" % l] = np.ascontiguousarray(inp["ple_w_gate"][l])
        w["wpp%d" % l] = np.ascontiguousarray(inp["ple_w_proj"][l])
    cols = []

    def pp(v):
        v = np.asarray(v, np.float32)
        cols.append(v.reshape(-1, 128).T)

    for l in range(DEPTH):
        pp(inp["ln_mix_g"][l]); pp(inp["ln_mix_b"][l]); pp(inp["ln_ffn_g"][l]); pp(inp["ln_ffn_b"][l])
        pp(inp["ffn_conv_w"][l][0]); pp(inp["ffn_conv_w"][l][1]); pp(inp["ffn_conv_w"][l][2]); pp(inp["ffn_conv_b"][l])
    pp(inp["ab_q_norm"][0]); pp(inp["ab_kv_norm"][0]); pp(inp["c_subln"][0])
    w["vec"] = np.ascontiguousarray(np.concatenate(cols, axis=1))
    w["lam"] = np.ascontiguousarray(np.asarray(inp["c_lambda"][0], np.float32).reshape(1, 256))
    return w


VEC_L = 120
V_QN = 2 * VEC_L
V_KVN = V_QN + 3
V_SUB = V_KVN + 2
NVEC = V_SUB + 1


def make_consts():
    c = {}
    c["ident"] = np.eye(128, dtype=np.float32)
    fa = (1.0 / (ROPE_THETA ** (np.arange(0, 16, 2, dtype=np.float32) / np.float32(16)))).astype(np.float32)
    fb = (1.0 / (ROPE_THETA ** (np.arange(0, 32, 2, dtype=np.float32) / np.float32(32)))).astype(np.float32)
    rc = np.zeros((128, 8), np.float32)
    for p in range(128):
        i = p % 64
        if i < 16:
            rc[p, 0] = fa[i % 8] / TWO_PI
            rc[p, 1] = -TWO_PI if i < 8 else TWO_PI
        if 64 <= p < 96:
            j = p - 64
            rc[p, 2] = fb[j % 16] / TWO_PI
            rc[p, 3] = -TWO_PI if j < 16 else TWO_PI
    rc[:, 4] = TWO_PI
    rc[:, 5] = LN_EPS
    rc[:, 6] = RMS_EPS
    c["rc"] = rc
    p = np.arange(128)[:, None]
    u = np.arange(MASK_W)[None, :]
    off = p - u + MASK_U0
    a = np.abs(off)
    m = (a <= 64).astype(np.float32) + ((off % 4 == 0) & (a <= 256)) + ((off % 16 == 0) & (a <= 1024))
    c["maskA"] = np.ascontiguousarray(m.astype(np.float32))
    rp = np.zeros((128, 128), np.float32)
    for m_ in range(128):
        i = m_ % 64
        k_ = m_ + 8 if i < 8 else (m_ - 8 if i < 16 else m_)
        rp[k_, m_] = 1.0
    c["rperm"] = rp
    return c


_MASK_NP = None


def mask_np():
    global _MASK_NP
    if _MASK_NP is None:
        _MASK_NP = make_consts()["maskA"]
    return _MASK_NP


def build_program(dbg=False, stop_after=None):
    nc = bass.Bass("TRN2", target_bir_lowering=False)

    def din(name, shape, dt=F32):
        return nc.dram_tensor(name, list(shape), dt, kind="ExternalInput").ap()

    def dscr(name, shape, dt):
        return nc.dram_tensor(name, list(shape), dt, kind="ExternalOutput" if dbg else "Internal").ap()

    x_in = din("x", [NT, D])
    p_in = din("p", [DEPTH, NT, 256])
    pos_in = din("pos", [NSEQ, T], I32)
    Wd_ = {}
    for name, shape in [("w0p", [D, 1856]), ("w0v", [D, 512]), ("wqu", [384, 1536]), ("wkn", [256, 512]),
                        ("wvb", [256, 512]), ("wo0", [D, D]), ("w1p", [D, 2048]), ("w1v", [D, 1024]),
                        ("wo1", [D, D]), ("vec", [128, NVEC]), ("lam", [1, 256]),
                        ("ident", [128, 128]), ("rperm", [128, 128]), ("rc", [128, 8]), ("maskA", [128, MASK_W])]:
        Wd_[name] = din(name, shape)
    for l in range(DEPTH):
        Wd_["wg%d" % l] = din("wg%d" % l, [D, DFF])
        Wd_["wu%d" % l] = din("wu%d" % l, [D, DFF])
        Wd_["wd%d" % l] = din("wd%d" % l, [DFF, D])
        Wd_["wpg%d" % l] = din("wpg%d" % l, [D, D])
        Wd_["wpp%d" % l] = din("wpp%d" % l, [256, D])
    out_d = nc.dram_tensor("out", [NT, D], F32, kind="ExternalOutput").ap()

    XF = dscr("XF", [8, 128, NT], F32)
    XB = dscr("XB", [8, 128, NT], BF16)
    X1B = dscr("X1B", [8, 128, NT], BF16)
    TAB = dscr("TAB", [4, 128, NT], F32)
    PT = dscr("PT", [DEPTH, 2, 128, NT], BF16)
    QA = dscr("QA", [4, 128, NT], BF16)
    KA = dscr("KA", [4, 128, NT], BF16)
    VA = dscr("VA", [NT, 512], BF16)
    QB = dscr("QB", [8, 96, NT], BF16)
    KB = dscr("KB", [8, 96, NT], BF16)
    VB = dscr("VB", [NT, 512], BF16)
    QC = dscr("QC", [8, 128, NT], BF16)
    KC = dscr("KC", [8, 128, NT], BF16)
    VC = dscr("VC", [NT, 1024], BF16)
    AO = dscr("AO", [8, 128, NT], BF16)

    es = ExitStack()
    S = Sched(nc)
    dbufs = {}

    def Dq(name, i):
        k = (name, i)
        if k not in dbufs:
            dbufs[k] = Buf(None, "%s%d" % (name, i))
        return dbufs[k]

    def sb(name, shape, dt):
        return Buf(es.enter_context(nc.sbuf_tensor("s_" + name, list(shape), dt)), name)

    PSB = [Buf(es.enter_context(nc.psum_tensor("ps%d" % i, [128, 512], F32)), "ps%d" % i) for i in range(8)]
    ps_all = Ring(PSB)

    ident = sb("ident", [128, 128], F32)
    ones = sb("ones", [128, 128], BF16)
    rperm = sb("rperm", [128, 128], BF16)
    vec = sb("vec", [128, NVEC], F32)
    rc = sb("rc", [128, 8], F32)
    lamb = sb("lamb", [128, 256], F32)
    lamt = sb("lamt", [128, 8], F32)
    gsub = sb("gsub", [128, 1], F32)
    f32t = Ring([sb("f32t%d" % i, [128, 512], F32) for i in range(8)])
    b16t = Ring([sb("b16t%d" % i, [128, 512], BF16) for i in range(8)])
    big32 = sb("big32", [128, 4104], F32)
    big32b = sb("big32b", [128, 4104], F32)
    for B_ in (big32, big32b):
        B_.ch = [Buf(B_.t, "%s_c%d" % (B_.name, c)) for c in range(8)]
    bigs = Ring([big32, big32b])

    def xf_view(B):
        return B[:, 0:4096].rearrange("p (c t) -> p c t", c=8)

    def xbh_view(B):
        return B[:, 0:4104].bitcast(BF16).rearrange("p (c t) -> p c t", c=8)
    bigb = Ring([sb("bigb%d" % i, [128, 8, 512], BF16) for i in range(3)])
    wring = Ring([sb("wring%d" % i, [128, 5632], BF16) for i in range(2)])
    wres = sb("wres", [128, 8192], BF16)
    wpp_t = sb("wpp", [128, 2, 1024], BF16)
    harena = sb("harena", [128, 22528], BF16)
    tabs = [sb("tab%d" % i, [128, 512], F32) for i in range(4)]
    wsm = Ring([sb("wsm%d" % i, [128, 2048], BF16) for i in range(6)])
    ptile = sb("ptile", [128, 2, 512], BF16)
    att = [Buf(harena.t, "att%d" % i) for i in range(6)]
    h_alias = att

    def V_(l, off, c):
        return vec[:, l * VEC_L + off + c: l * VEC_L + off + c + 1]

    ACT, DVE, POOL, PE, SP = "act", "dve", "pool", "pe", "sp"

    S.op(SP, lambda e: e.dma_start(out=ident[:], in_=Wd_["ident"]), writes=[ident], dma=True)
    S.op(SP, lambda e: e.dma_start(out=vec[:], in_=Wd_["vec"]), writes=[vec], dma=True)
    S.op(SP, lambda e: e.dma_start(out=rc[:], in_=Wd_["rc"]), writes=[rc], dma=True)
    S.op(SP, lambda e: e.dma_start(out=lamb[:], in_=Wd_["lam"].partition_broadcast(128)), writes=[lamb], dma=True)
    S.op(DVE, lambda e: e.memset(ones[:], 1.0), writes=[ones])
    S.op(POOL, lambda e: e.dma_start(out=rperm[:], in_=Wd_["rperm"]), writes=[rperm], dma=True)
    lambda_init = 0.8 - 0.6 * math.exp(-0.3 * 1)
    t_ = f32t.next()
    S.op(DVE, lambda e: e.tensor_tensor(out=t_[:, 0:64], in0=lamb[:, 0:64], in1=lamb[:, 64:128], op=ALU.mult), reads=[lamb], writes=[t_])
    S.op(DVE, lambda e: e.tensor_tensor(out=t_[:, 64:128], in0=lamb[:, 128:192], in1=lamb[:, 192:256], op=ALU.mult), reads=[lamb], accum=[t_])
    S.op(DVE, lambda e: e.reduce_sum(out=lamt[:, 0:1], in_=t_[:, 0:64], axis=mybir.AxisListType.X), reads=[t_], writes=[lamt])
    S.op(DVE, lambda e: e.reduce_sum(out=lamt[:, 1:2], in_=t_[:, 64:128], axis=mybir.AxisListType.X), reads=[t_], accum=[lamt])
    S.op(ACT, lambda e: e.activation(out=lamt[:, 2:4], in_=lamt[:, 0:2], func=AF.Exp), reads=[lamt], accum=[lamt])
    S.op(DVE, lambda e: e.tensor_tensor(out=lamt[:, 4:5], in0=lamt[:, 3:4], in1=lamt[:, 2:3], op=ALU.subtract), reads=[lamt], accum=[lamt])
    S.op(DVE, lambda e: e.tensor_scalar(out=lamt[:, 5:6], in0=lamt[:, 4:5], scalar1=float(-lambda_init), scalar2=None, op0=ALU.add), reads=[lamt], accum=[lamt])
    nlam = lamt[:, 5:6]
    S.op(DVE, lambda e: e.tensor_scalar(out=gsub[:], in0=vec[:, V_SUB:V_SUB + 1], scalar1=float(1.0 - lambda_init), scalar2=None, op0=ALU.mult), reads=[vec], writes=[gsub])

    import os as _os
    _lim = _os.environ.get("K_LIM", "")
    posf = big32[:, 0:2048]
    posi = big32[:, 2048:4096].bitcast(I32)
    for s in range(NSEQ if _lim != "setup" else 0):
        S.op(SP, lambda e, s=s: e.dma_start(out=posi, in_=pos_in[s:s + 1, :].partition_broadcast(128)), writes=big32.ch, dma=True)
        S.op(DVE, lambda e: e.tensor_copy(out=posf, in_=posi), reads=big32.ch, accum=big32.ch)
        for kind in range(4):
            fcol = 0 if kind < 2 else 2
            scol = 4 if kind % 2 == 0 else (1 if kind == 1 else 3)
            ph = 0.25 if kind % 2 == 0 else 0.0
            for tl in range(4):
                a = f32t.next(); b = f32t.next(); c = f32t.next()
                sl = slice(tl * 512, (tl + 1) * 512)
                S.op(DVE, lambda e, a=a, sl=sl, fcol=fcol, ph=ph: e.tensor_scalar(out=a[:], in0=posf[:, sl], scalar1=rc[:, fcol:fcol + 1], scalar2=float(ph), op0=ALU.mult, op1=ALU.add), reads=big32.ch + [rc], writes=[a])
                S.op(DVE, lambda e, a=a, b=b: e.tensor_copy(out=b[:].bitcast(I32), in_=a[:]), reads=[a], writes=[b])
                S.op(DVE, lambda e, b=b, c=c: e.tensor_copy(out=c[:], in_=b[:].bitcast(I32)), reads=[b], writes=[c])
                S.op(DVE, lambda e, a=a, c=c: e.tensor_tensor(out=a[:], in0=a[:], in1=c[:], op=ALU.subtract), reads=[a, c], writes=[a])
                S.op(DVE, lambda e, a=a, c=c: e.tensor_scalar(out=c[:], in0=a[:], scalar1=0.5, scalar2=None, op0=ALU.is_gt), reads=[a], writes=[c])
                S.op(DVE, lambda e, a=a, c=c: e.tensor_tensor(out=a[:], in0=a[:], in1=c[:], op=ALU.subtract), reads=[a, c], writes=[a])
                S.op(DVE, lambda e, a=a, c=c: e.tensor_scalar(out=c[:], in0=a[:], scalar1=-0.5, scalar2=None, op0=ALU.is_lt), reads=[a], writes=[c])
                S.op(DVE, lambda e, a=a, c=c: e.tensor_tensor(out=a[:], in0=a[:], in1=c[:], op=ALU.add), reads=[a, c], writes=[a])
                S.op(ACT, lambda e, a=a, b=b, scol=scol: e.activation(out=b[:], in_=a[:], func=AF.Sin, scale=rc[:, scol:scol + 1]), reads=[a, rc], writes=[b])
                g = s * 4 + tl
                S.op(SP, lambda e, b=b, kind=kind, g=g: e.dma_start(out=TAB[kind, :, g * 512:(g + 1) * 512], in_=b[:]), reads=[b], accum=[Dq("TAB", g)], dma=True)

    for g in range(int(_os.environ.get("K_XG", "8")) if _lim not in ("setup", "tabs") else 0):
        BX = bigs.next()
        xin4 = BX[:, 0:4096].rearrange("p (j d) -> p j d", j=4)
        S.op(SP, lambda e, g=g, xin4=xin4: e.dma_start(out=xin4, in_=x_in[g * 512:(g + 1) * 512, :].rearrange("(j p) d -> p j d", p=128)), writes=BX.ch, dma=True)
        for c in range(8):
            P = ps_all.next()

            def tr(e, P=P, c=c, xin4=xin4):
                for j in range(4):
                    ins = e.transpose(out=P[:, j * 128:(j + 1) * 128], in_=xin4[:, j, c * 128:(c + 1) * 128], identity=ident[:])
                return ins
            _xs = int(_os.environ.get("K_XS", "3"))
            if _xs >= 1:
                S.op(PE, tr, reads=BX.ch + [ident], writes=[P])
            f = f32t.next(); bb = b16t.next()
            if _xs >= 2:
                _xc = _os.environ.get("K_XC", "both")
                if _xc in ("both", "act"):
                    S.op(ACT, lambda e, f=f, P=P: e.copy(out=f[:], in_=P[:]), reads=[P], writes=[f])
                if _xc in ("both", "dve"):
                    S.op(DVE, lambda e, bb=bb, f=f: e.tensor_copy(out=bb[:], in_=f[:]), reads=[f], writes=[bb])
            if _xs >= 3:
                S.op(ACT, lambda e, f=f, c=c, g=g: e.dma_start(out=XF[c, :, g * 512:(g + 1) * 512], in_=f[:]), reads=[f], accum=[Dq("XF", g)], dma=True)
                S.op(ACT, lambda e, bb=bb, c=c, g=g: e.dma_start(out=XB[c, :, g * 512:(g + 1) * 512], in_=bb[:]), reads=[bb], accum=[Dq("XB", g)], dma=True)
        for l in range(DEPTH if _os.environ.get("K_XP", "1") == "1" else 0):
            BP = bigs.next()
            pin4 = BP[:, 0:1024].rearrange("p (j d) -> p j d", j=4)
            S.op(SP, lambda e, g=g, l=l, pin4=pin4: e.dma_start(out=pin4, in_=p_in[l, g * 512:(g + 1) * 512, :].rearrange("(j p) d -> p j d", p=128)), writes=BP.ch, dma=True)
            for k in range(2):
                P = ps_all.next()

                def trp(e, P=P, k=k, pin4=pin4):
                    for j in range(4):
                        ins = e.transpose(out=P[:, j * 128:(j + 1) * 128], in_=pin4[:, j, k * 128:(k + 1) * 128], identity=ident[:])
                    return ins
                S.op(PE, trp, reads=BP.ch + [ident], writes=[P])
                bb = b16t.next()
                S.op(ACT, lambda e, bb=bb, P=P: e.copy(out=bb[:], in_=P[:]), reads=[P], writes=[bb])
                S.op(ACT, lambda e, bb=bb, l=l, k=k, g=g: e.dma_start(out=PT[l, k, :, g * 512:(g + 1) * 512], in_=bb[:]), reads=[bb], accum=[Dq("PT%d" % l, g)], dma=True)

    def load_w(dst_ap, src_ap, buf):
        S.op(POOL, lambda e: e.dma_start(out=dst_ap, in_=src_ap), writes=[buf], dma=True)

    def mm_group(P, M, N, pairs, extra_reads, pcol0=0):
        def f(e):
            n = len(pairs)
            for i, (l, r) in enumerate(pairs):
                ins = e.matmul(P[0:M, pcol0:pcol0 + N], lhsT=l, rhs=r, start=(i == 0), stop=(i == n - 1))
            return ins
        S.op(PE, f, reads=extra_reads, writes=[P])

    def rope_store(P1, P2, M, Ct, St, dst_ap, dst_buf, p0=0):
        t1 = f32t.next(); t2 = f32t.next(); o = b16t.next()
        S.op(DVE, lambda e: e.tensor_tensor(out=t1[p0:p0 + M, :], in0=P1[p0:p0 + M, :], in1=Ct[p0:p0 + M, :], op=ALU.mult), reads=[P1, Ct], writes=[t1])
        S.op(DVE, lambda e: e.tensor_tensor(out=t2[p0:p0 + M, :], in0=P2[p0:p0 + M, :], in1=St[p0:p0 + M, :], op=ALU.mult), reads=[P2, St], writes=[t2])
        S.op(POOL, lambda e: e.tensor_tensor(out=o[p0:p0 + M, :], in0=t1[p0:p0 + M, :], in1=t2[p0:p0 + M, :], op=ALU.add), reads=[t1, t2], writes=[o])
        if isinstance(dst_ap, list):
            for d in dst_ap:
                S.op(SP, lambda e, d=d: e.dma_start(out=d, in_=o[p0:p0 + M, :]), reads=[o], accum=[dst_buf], dma=True)
        else:
            S.op(SP, lambda e: e.dma_start(out=dst_ap, in_=o[p0:p0 + M, :]), reads=[o], accum=[dst_buf], dma=True)

    def rope_store_r(P1, Ct, St, dst_ap, dst_buf):
        qb_ = b16t.next()
        S.op(ACT, lambda e: e.copy(out=qb_[:], in_=P1[:]), reads=[P1], writes=[qb_])
        P2 = ps_all.next()
        mm_group(P2, 128, 512, [(rperm[:], qb_[:])], [rperm, qb_])
        t1 = f32t.next(); t2 = f32t.next(); o = b16t.next()
        S.op(DVE, lambda e: e.tensor_tensor(out=t1[:], in0=P1[:], in1=Ct[:], op=ALU.mult), reads=[P1, Ct, qb_], writes=[t1])
        S.op(DVE, lambda e: e.tensor_tensor(out=t2[:], in0=P2[:], in1=St[:], op=ALU.mult), reads=[P2, St], writes=[t2])
        S.op(POOL, lambda e: e.tensor_tensor(out=o[:], in0=t1[:], in1=t2[:], op=ALU.add), reads=[t1, t2], writes=[o])
        S.op(SP, lambda e: e.dma_start(out=dst_ap, in_=o[:]), reads=[o], accum=[dst_buf], dma=True)

    def load_tabs(g, kinds, slot):
        for n_, k in enumerate(kinds):
            S.op(SP, lambda e, k=k, n_=n_: e.dma_start(out=tabs[2 * slot + n_][:], in_=TAB[k, :, g * 512:(g + 1) * 512]), reads=[Dq("TAB", g)], writes=[tabs[2 * slot + n_]], dma=True)
        return tabs[2 * slot], tabs[2 * slot + 1]

    def rstd_from_ss(Pss, scale, epscol):
        r = f32t.next()
        S.op(ACT, lambda e: e.activation(out=r[:], in_=Pss[:], func=AF.Ln, bias=rc[:, epscol:epscol + 1], scale=float(scale)), reads=[Pss, rc], writes=[r])
        S.op(ACT, lambda e: e.activation(out=r[:], in_=r[:], func=AF.Exp, scale=-0.5), reads=[r], writes=[r])
        return r

    def phase1_layer0():
        W = Wd_["w0p"]
        wqu = wres[:, 0:4608].rearrange("p (k n) -> p k n", k=3)
        wkn = wres[:, 4608:5632].rearrange("p (k n) -> p k n", k=2)
        wvb = wres[:, 5632:6656].rearrange("p (k n) -> p k n", k=2)
        load_w(wqu, Wd_["wqu"].rearrange("(k p) n -> p k n", p=128), wres)
        S.op(POOL, lambda e: e.dma_start(out=wkn, in_=Wd_["wkn"].rearrange("(k p) n -> p k n", p=128)), accum=[wres], dma=True)
        S.op(POOL, lambda e: e.dma_start(out=wvb, in_=Wd_["wvb"].rearrange("(k p) n -> p k n", p=128)), accum=[wres], dma=True)

        def load_x(g):
            x = bigb.next()
            S.op(SP, lambda e: e.dma_start(out=x[:], in_=XB[:, :, g * 512:(g + 1) * 512].rearrange("c p t -> p c t")), reads=[Dq("XB", g)], writes=[x], dma=True)
            return x

        def proj(wv, wb, x, col, M):
            P = ps_all.next()
            mm_group(P, M, 512, [(wv[:, k, col:col + M], x[:, k, :]) for k in range(8)], [wb, x])
            return P

        def evac_f32(P):
            f = f32t.next()
            S.op(ACT, lambda e: e.copy(out=f[:], in_=P[:]), reads=[P], writes=[f])
            return f

        def store_copy(P, M, dst_ap, dbuf):
            o = b16t.next()
            S.op(ACT, lambda e: e.copy(out=o[0:M, :], in_=P[0:M, :]), reads=[P], writes=[o])
            S.op(SP, lambda e: e.dma_start(out=dst_ap, in_=o[0:M, :]), reads=[o], accum=[dbuf], dma=True)

        def latent_tile(g, x, wv4, wb4, wv5, wb5):
            tsl = slice(g * 512, (g + 1) * 512)
            lat = bigb.next()
            raw = [evac_f32(proj(wv4, wb4, x, cc * 128, 128)) for cc in range(4)]
            raw.append(evac_f32(proj(wv5, wb5, x, 0, 128)))
            for (idxs, vcol, nfeat) in (((0, 1, 2), V_QN, 384), ((3, 4), V_KVN, 256)):
                sq = []
                for ci in idxs:
                    q = b16t.next()
                    S.op(ACT, lambda e, q=q, f=raw[ci]: e.activation(out=q[:], in_=f[:], func=AF.Square), reads=[raw[ci]], writes=[q])
                    sq.append(q)
                Pss = ps_all.next()
                mm_group(Pss, 128, 512, [(ones[:], q[:]) for q in sq], [ones] + sq)
                r = rstd_from_ss(Pss, 1.0 / nfeat, 6)
                for n_, ci in enumerate(idxs):
                    S.op(DVE, lambda e, ci=ci, n_=n_, vcol=vcol, r=r: e.scalar_tensor_tensor(out=lat[:, ci, :], in0=raw[ci][:], scalar=vec[:, vcol + n_:vcol + n_ + 1], in1=r[:], op0=ALU.mult, op1=ALU.mult), reads=[raw[ci], r, vec], accum=[lat])
            tC, tS = load_tabs(g, [2, 3], g % 2)
            P1 = proj(wv5, wb5, x, 128, 96)
            P2 = proj(wv5, wb5, x, 224, 96)
            rope_store(P1, P2, 32, tC, tS, [KB[h, 64:96, tsl] for h in range(8)], Dq("KB", g), p0=64)
            for h in range(8):
                P1 = ps_all.next(); P2 = ps_all.next()
                mm_group(P1, 96, 512, [(wqu[:, k, h * 192:h * 192 + 96], lat[:, k, :]) for k in range(3)], [wres, lat])
                mm_group(P2, 96, 512, [(wqu[:, k, h * 192 + 96:h * 192 + 192], lat[:, k, :]) for k in range(3)], [wres, lat])
                rope_store(P1, P2, 96, tC, tS, QB[h, :, tsl], Dq("QB", g))
            for h in range(8):
                P = ps_all.next()
                mm_group(P, 64, 512, [(wkn[:, k, h * 64:(h + 1) * 64], lat[:, 3 + k, :]) for k in range(2)], [wres, lat])
                store_copy(P, 64, KB[h, 0:64, tsl], Dq("KB", g))
            for j in range(4):
                P = ps_all.next()
                mm_group(P, 128, 512, [(lat[:, 3 + k, j * 128:(j + 1) * 128], wvb[:, k, :]) for k in range(2)], [wres, lat])
                store_copy(P, 128, VB[g * 512 + j * 128:g * 512 + (j + 1) * 128, :], Dq("VB", g))

        for blk in range(4):
            xt = [load_x(blk * 2 + i) for i in range(2)]
            tA = [load_tabs(blk * 2 + i, [0, 1], i) for i in range(2)]
            pend = []
            for gi in range(4):
                wb = wsm.next()
                wv = wb[:].rearrange("p (k n) -> p k n", k=8)
                load_w(wv, W[:, gi * 256:(gi + 1) * 256].rearrange("(k p) n -> p k n", p=128), wb)
                for i in range(2):
                    g = blk * 2 + i
                    tsl = slice(g * 512, (g + 1) * 512)
                    dst = QA if gi < 2 else KA
                    nm = "QA" if gi < 2 else "KA"
                    for jj in range(2):
                        j = (gi % 2) * 2 + jj
                        P1 = proj(wv, wb, xt[i], jj * 128, 128)
                        if pend:
                            rope_store_r(*pend.pop())
                        pend.append((P1, tA[i][0], tA[i][1], dst[j, :, tsl], Dq(nm, g)))
            if pend:
                rope_store_r(*pend.pop())
            wb = wring.next()
            wv = wb[:, 0:4096].rearrange("p (k n) -> p k n", k=8)
            load_w(wv, Wd_["w0v"].rearrange("(k p) n -> p k n", p=128), wb)
            for i in range(2):
                g = blk * 2 + i
                for j in range(4):
                    P = ps_all.next()
                    mm_group(P, 128, 512, [(xt[i][:, k, j * 128:(j + 1) * 128], wv[:, k, :]) for k in range(8)], [wb, xt[i]])
                    store_copy(P, 128, VA[g * 512 + j * 128:g * 512 + (j + 1) * 128, :], Dq("VA", g))
            wb4 = wring.next(); wb5 = wring.next()
            wv4 = wb4[:, 0:4096].rearrange("p (k n) -> p k n", k=8)
            wv5 = wb5[:, 0:2560].rearrange("p (k n) -> p k n", k=8)
            load_w(wv4, W[:, 1024:1536].rearrange("(k p) n -> p k n", p=128), wb4)
            load_w(wv5, W[:, 1536:1856].rearrange("(k p) n -> p k n", p=128), wb5)
            for i in range(2):
                latent_tile(blk * 2 + i, xt[i], wv4, wb4, wv5, wb5)

    lat_state = {}

    def phase1_layer1():
        W = Wd_["w1p"]
        for blk in range(4):
            xt = [bigb.next(), bigb.next()]
            for i in range(2):
                g = blk * 2 + i
                S.op(SP, lambda e, x_=xt[i], g=g: e.dma_start(out=x_[:], in_=XB[:, :, g * 512:(g + 1) * 512].rearrange("c p t -> p c t")), reads=[Dq("XB", g)], writes=[xt[i]], dma=True)
            tA = [load_tabs(blk * 2 + i, [0, 1], i) for i in range(2)]
            pend = []
            for gi in range(8):
                wb = wsm.next()
                wv = wb[:].rearrange("p (k n) -> p k n", k=8)
                load_w(wv, W[:, gi * 256:(gi + 1) * 256].rearrange("(k p) n -> p k n", p=128), wb)
                dst = QC if gi < 4 else KC
                nm = "QC" if gi < 4 else "KC"
                for i in range(2):
                    g = blk * 2 + i
                    tsl = slice(g * 512, (g + 1) * 512)
                    for jj in range(2):
                        j = (gi % 4) * 2 + jj
                        P1 = ps_all.next()
                        mm_group(P1, 128, 512, [(wv[:, k, jj * 128:jj * 128 + 128], xt[i][:, k, :]) for k in range(8)], [wb, xt[i]])
                        if pend:
                            rope_store_r(*pend.pop())
                        pend.append((P1, tA[i][0], tA[i][1], dst[j, :, tsl], Dq(nm, g)))
            if pend:
                rope_store_r(*pend.pop())
            for half in range(2):
                wb = wring.next()
                wv = wb[:, 0:4096].rearrange("p (k n) -> p k n", k=8)
                load_w(wv, Wd_["w1v"][:, half * 512:(half + 1) * 512].rearrange("(k p) n -> p k n", p=128), wb)
                for i in range(2):
                    g = blk * 2 + i
                    for j in range(4):
                        P = ps_all.next()
                        mm_group(P, 128, 512, [(xt[i][:, k, j * 128:(j + 1) * 128], wv[:, k, :]) for k in range(8)], [wb, xt[i]])
                        o = b16t.next()
                        S.op(ACT, lambda e, o=o, P=P: e.copy(out=o[:], in_=P[:]), reads=[P], writes=[o])
                        S.op(SP, lambda e, o=o, j=j, g=g, half=half: e.dma_start(out=VC[g * 512 + j * 128:g * 512 + (j + 1) * 128, half * 512:(half + 1) * 512], in_=o[:]), reads=[o], accum=[Dq("VC", g)], dma=True)

    psS = Ring(PSB[0:4])
    psO = Ring(PSB[4:6])
    att_slot = [0]

    def attention(kind):
        mk = mask_np()

        if kind == "A":
            for c0 in range(0, MASK_W, 496):
                S.op(POOL, lambda e, c0=c0: e.dma_start(out=wres[:, c0:c0 + 496], in_=Wd_["maskA"][:, c0:c0 + 496]), writes=([wres] if c0 == 0 else []), accum=([] if c0 == 0 else [wres]), dma=True)

        def load_head(s, h):
            seqbufs = [g for g in range(s * 4, s * 4 + 4)]
            sl = att_slot[0]
            att_slot[0] = (sl + 1) % 2
            kT = harena[:, (sl * 3 + 0) * 2048:(sl * 3 + 1) * 2048]
            qT = harena[:, (sl * 3 + 1) * 2048:(sl * 3 + 2) * 2048]
            Vt = harena[:, (sl * 3 + 2) * 2048:(sl * 3 + 3) * 2048].rearrange("p (c d) -> p c d", c=16)
            kb, qb, vb = att[sl * 3], att[sl * 3 + 1], att[sl * 3 + 2]
            tok = slice(s * T, (s + 1) * T)
            if kind == "A":
                dq, scale = 64, 64 ** -0.5
                p0 = (h % 2) * 64
                ksrc, qsrc = KA[h // 2, p0:p0 + 64, tok], QA[h // 2, p0:p0 + 64, tok]
                vsrc = VA[tok, h * 64:(h + 1) * 64].rearrange("(c p) d -> p c d", p=128)
                names = ("KA", "QA", "VA")
                dv = 64
            elif kind == "B":
                dq, scale = 96, 96 ** -0.5
                ksrc, qsrc = KB[h, :, tok], QB[h, :, tok]
                vsrc = VB[tok, h * 64:(h + 1) * 64].rearrange("(c p) d -> p c d", p=128)
                names = ("KB", "QB", "VB")
                dv = 64
            else:
                dq, scale = 128, 64 ** -0.5
                ksrc, qsrc = KC[h, :, tok], QC[h, :, tok]
                vsrc = VC[tok, h * 128:(h + 1) * 128].rearrange("(c p) d -> p c d", p=128)
                names = ("KC", "QC", "VC")
                dv = 128
            S.op(SP, lambda e: e.dma_start(out=kT[0:dq, :], in_=ksrc), reads=[Dq(names[0], g) for g in seqbufs], writes=[kb], dma=True)
            S.op(SP, lambda e: e.dma_start(out=qT[0:dq, :], in_=qsrc), reads=[Dq(names[1], g) for g in seqbufs], writes=[qb], dma=True)
            S.op(SP, lambda e: e.dma_start(out=Vt[:, :, 0:dv], in_=vsrc), reads=[Dq(names[2], g) for g in seqbufs], writes=[vb], dma=True)
            if dv == 64:
                S.op(POOL, lambda e: e.memset(Vt[:, :, 64:128], 1.0), accum=[vb])
            return (kT, qT, Vt, kb, qb, vb, dq, scale, dv)

        def do_head(s, h, ctx):
            kT, qT, Vt, kb, qb, vb, dq, scale, dv = ctx

            def do_qt_ab(qt):
                qsl = slice(qt * 512, (qt + 1) * 512)
                g = s * 4 + qt
                if kind == "A":
                    kcs = [kc for kc in range(16) if mk[:, qt * 512 - kc * 128 + MASK_U0: qt * 512 - kc * 128 + MASK_U0 + 512].any()]
                else:
                    kcs = list(range(16))
                n = len(kcs)
                O = psO.next()
                Eb = {}

                def issue_s(i):
                    kc = kcs[i]
                    P = psS.next()
                    mm_group(P, 128, 512, [(kT[0:dq, kc * 128:(kc + 1) * 128], qT[0:dq, qsl])], [kb, qb])
                    E = b16t.next()
                    S.op(ACT, lambda e: e.activation(out=E[:], in_=P[:], func=AF.Exp, scale=float(scale)), reads=[P], writes=[E])
                    if kind == "A":
                        base = qt * 512 - kc * 128 + MASK_U0
                        S.op(DVE, lambda e: e.tensor_tensor(out=E[:], in0=E[:], in1=wres[:, base:base + 512], op=ALU.mult), reads=[E, wres], writes=[E])
                    Eb[i] = E

                def issue_pv(i):
                    kc = kcs[i]
                    E = Eb.pop(i)
                    S.op(PE, lambda e: e.matmul(O[:], lhsT=Vt[:, kc, :], rhs=E[:], start=(i == 0), stop=(i == n - 1)), reads=[vb, E], writes=[O])
                LOOK = 3
                for i in range(min(LOOK, n)):
                    issue_s(i)
                for i in range(n):
                    if i + LOOK < n:
                        issue_s(i + LOOK)
                    issue_pv(i)
                R = f32t.next(); o = b16t.next()
                S.op(ACT, lambda e: e.activation(out=R[0:64, :], in_=O[64:128, :], func=AF.Ln), reads=[O], writes=[R])
                S.op(ACT, lambda e: e.activation(out=R[0:64, :], in_=R[0:64, :], func=AF.Exp, scale=-1.0), reads=[R], writes=[R])
                S.op(DVE, lambda e: e.tensor_tensor(out=o[0:64, :], in0=O[0:64, :], in1=R[0:64, :], op=ALU.mult), reads=[O, R], writes=[o])
                mixrow = (0 if kind == "A" else 512) + h * 64
                c_, r_ = mixrow // 128, mixrow % 128
                S.op(SP, lambda e: e.dma_start(out=AO[c_, r_:r_ + 64, g * 512:(g + 1) * 512], in_=o[0:64, :]), reads=[o], accum=[Dq("AO", g)], dma=True)

            def do_qt_c(qt):
                qsl = slice(qt * 512, (qt + 1) * 512)
                g = s * 4 + qt
                n = 16
                U0, U1, L0, L1 = PSB[4], PSB[5], PSB[6], PSB[7]
                Eb = {}

                def issue_s(i):
                    Es = []
                    for c in range(2):
                        P = psS.next()
                        mm_group(P, 128, 512, [(kT[c * 64:(c + 1) * 64, i * 128:(i + 1) * 128], qT[c * 64:(c + 1) * 64, qsl])], [kb, qb])
                        E = b16t.next()
                        S.op(ACT, lambda e, E=E, P=P: e.activation(out=E[:], in_=P[:], func=AF.Exp, scale=float(scale)), reads=[P], writes=[E])
                        Es.append(E)
                    Eb[i] = Es

                def issue_pv(i):
                    Es = Eb.pop(i)
                    for c, (U, L) in enumerate(((U0, L0), (U1, L1))):
                        E = Es[c]
                        S.op(PE, lambda e, U=U, E=E: e.matmul(U[:], lhsT=Vt[:, i, :], rhs=E[:], start=(i == 0), stop=(i == n - 1)), reads=[vb, E], writes=[U])
                        S.op(PE, lambda e, L=L, E=E: e.matmul(L[:], lhsT=ones[:], rhs=E[:], start=(i == 0), stop=(i == n - 1)), reads=[ones, E], writes=[L])
                issue_s(0)
                for i in range(n):
                    if i + 1 < n:
                        issue_s(i + 1)
                    issue_pv(i)
                r0 = f32t.next(); r1 = f32t.next(); t0 = f32t.next(); t1 = f32t.next(); osq = b16t.next(); o = b16t.next()
                S.op(ACT, lambda e: e.activation(out=r0[:], in_=L0[:], func=AF.Ln), reads=[L0], writes=[r0])
                S.op(ACT, lambda e: e.activation(out=r1[:], in_=L1[:], func=AF.Ln), reads=[L1], writes=[r1])
                S.op(DVE, lambda e: e.tensor_copy(out=t0[:], in_=U0[:]), reads=[U0], writes=[t0])
                S.op(DVE, lambda e: e.tensor_copy(out=t1[:], in_=U1[:]), reads=[U1], writes=[t1])
                S.op(ACT, lambda e: e.activation(out=r0[:], in_=r0[:], func=AF.Exp, scale=-1.0), reads=[r0], writes=[r0])
                S.op(ACT, lambda e: e.activation(out=r1[:], in_=r1[:], func=AF.Exp, scale=-1.0), reads=[r1], writes=[r1])
                S.op(DVE, lambda e: e.tensor_tensor(out=t0[:], in0=t0[:], in1=r0[:], op=ALU.mult), reads=[t0, r0], writes=[t0])
                S.op(DVE, lambda e: e.tensor_tensor(out=t1[:], in0=t1[:], in1=r1[:], op=ALU.mult), reads=[t1, r1], writes=[t1])
                S.op(DVE, lambda e: e.scalar_tensor_tensor(out=t0[:], in0=t1[:], scalar=nlam, in1=t0[:], op0=ALU.mult, op1=ALU.add), reads=[t0, t1, lamt], writes=[t0])
                S.op(ACT, lambda e: e.activation(out=osq[:], in_=t0[:], func=AF.Square), reads=[t0], writes=[osq])
                Pss = psS.next()
                mm_group(Pss, 128, 512, [(ones[:], osq[:])], [ones, osq])
                r = rstd_from_ss(Pss, 1.0 / 128, 5)
                S.op(DVE, lambda e: e.scalar_tensor_tensor(out=o[:], in0=t0[:], scalar=gsub[:, 0:1], in1=r[:], op0=ALU.mult, op1=ALU.mult), reads=[t0, r, gsub], writes=[o])
                S.op(SP, lambda e: e.dma_start(out=AO[h, :, g * 512:(g + 1) * 512], in_=o[:]), reads=[o], accum=[Dq("AO", g)], dma=True)

            for qt in range(4):
                if kind == "C":
                    do_qt_c(qt)
                else:
                    do_qt_ab(qt)

        order = [(s, h) for s in range(NSEQ) for h in range(8)]
        ctxs = {0: load_head(*order[0])}
        for n_, (s, h) in enumerate(order):
            if n_ + 1 < len(order):
                ctxs[n_ + 1] = load_head(*order[n_ + 1])
            do_head(s, h, ctxs.pop(n_))

    def layer_norm_gen(xf8, X, l, goff, boff, outb, outbuf, res):
        zb = bigb.next(); zq = bigb.next()
        res["zq"] = zq
        for c in range(8):
            S.op(ACT, lambda e, c=c: e.copy(out=zb[:, c, :], in_=xf8[:, c, :]), reads=[X.ch[c]], accum=[zb])
            S.op(ACT, lambda e, c=c: e.activation(out=zq[:, c, :], in_=xf8[:, c, :], func=AF.Square), reads=[X.ch[c]], accum=[zq])
        yield
        S1 = ps_all.next(); S2 = ps_all.next()
        mm_group(S1, 128, 512, [(ones[:], zb[:, c, :]) for c in range(8)], [ones, zb])
        mm_group(S2, 128, 512, [(ones[:], zq[:, c, :]) for c in range(8)], [ones, zq])
        m, msq, var, nmr = tabs[0], tabs[1], tabs[2], tabs[3]
        S.op(ACT, lambda e: e.mul(out=m[:], in_=S1[:], mul=1.0 / D), reads=[S1], writes=[m])
        S.op(DVE, lambda e: e.tensor_tensor(out=msq[:], in0=m[:], in1=m[:], op=ALU.mult), reads=[m], writes=[msq])
        S.op(DVE, lambda e: e.scalar_tensor_tensor(out=var[:], in0=S2[:], scalar=1.0 / D, in1=msq[:], op0=ALU.mult, op1=ALU.subtract), reads=[S2, msq], writes=[var])
        S.op(ACT, lambda e: e.activation(out=var[:], in_=var[:], func=AF.Ln, bias=rc[:, 5:6], scale=1.0), reads=[var, rc], writes=[var])
        S.op(ACT, lambda e: e.activation(out=var[:], in_=var[:], func=AF.Exp, scale=-0.5), reads=[var], writes=[var])
        S.op(DVE, lambda e: e.scalar_tensor_tensor(out=nmr[:], in0=m[:], scalar=-1.0, in1=var[:], op0=ALU.mult, op1=ALU.mult), reads=[m, var], writes=[nmr])
        yield
        for c in range(8):
            t = f32t.next()
            S.op(DVE, lambda e, c=c, t=t: e.scalar_tensor_tensor(out=t[:], in0=xf8[:, c, :], scalar=V_(l, goff, c), in1=var[:], op0=ALU.mult, op1=ALU.mult), reads=[X.ch[c], var, vec], writes=[t])
            S.op(DVE, lambda e, c=c, t=t: e.scalar_tensor_tensor(out=xf8[:, c, :], in0=nmr[:], scalar=V_(l, goff, c), in1=t[:], op0=ALU.mult, op1=ALU.add), reads=[nmr, vec, t], writes=[X.ch[c]])
            S.op(ACT, lambda e, c=c: e.activation(out=outb[:, c, :], in_=xf8[:, c, :], func=AF.Identity, bias=V_(l, boff, c), scale=1.0), reads=[X.ch[c], vec], accum=[outbuf])
            S.op(ACT, lambda e, c=c: e.activation(out=xf8[:, c, :], in_=xf8[:, c, :], func=AF.Identity, bias=V_(l, boff, c), scale=1.0), reads=[X.ch[c], vec], writes=[X.ch[c]])

    def layer_norm(xf8, X, l, goff, boff, outb, outbuf):
        res = {}
        for _ in layer_norm_gen(xf8, X, l, goff, boff, outb, outbuf, res):
            pass
        return res["zq"]

    def interleave(ga, gb):
        live = [gb, ga]
        while live:
            for g_ in list(live):
                try:
                    next(g_)
                except StopIteration:
                    live.remove(g_)

    def phase3a(l):
        wo = wres[:].rearrange("p (k n) -> p k n", k=8)
        load_w(wo, Wd_["wo%d" % l].rearrange("(k p) n -> p k n", p=128), wres)

        def tile3a_A(g, st):
            gsl = slice(g * 512, (g + 1) * 512)
            B = bigs.next()
            xf8 = xf_view(B)
            st[g] = (B, xf8)
            ao = bigb.next()
            S.op(SP, lambda e: e.dma_start(out=ao[:], in_=AO[:, :, gsl].rearrange("c p t -> p c t")), reads=[Dq("AO", g)], writes=[ao], dma=True)
            S.op(SP, lambda e: e.dma_start(out=xf8, in_=XF[:, :, gsl].rearrange("c p t -> p c t")), reads=[Dq("XF", g)], writes=B.ch, dma=True)
            for c in range(8):
                P = ps_all.next()
                mm_group(P, 128, 512, [(wo[:, k, c * 128:(c + 1) * 128], ao[:, k, :]) for k in range(8)], [wres, ao])
                S.op(DVE, lambda e, c=c, P=P: e.scalar_tensor_tensor(out=xf8[:, c, :], in0=xf8[:, c, :], scalar=ALPHA, in1=P[:], op0=ALU.mult, op1=ALU.add), reads=[B.ch[c], P], writes=[B.ch[c]])
                if c == 3:
                    yield

        def tile3a_B(g, B, xf8):
            gsl = slice(g * 512, (g + 1) * 512)
            ob = bigb.next()
            yield from layer_norm_gen(xf8, B, l, 0, 8, ob, ob, {})
            S.op(ACT, lambda e: e.dma_start(out=XF[:, :, gsl].rearrange("c p t -> p c t"), in_=xf8), reads=B.ch, writes=[Dq("XF", g)], dma=True)
            S.op(ACT, lambda e: e.dma_start(out=X1B[:, :, gsl].rearrange("c p t -> p c t"), in_=ob[:]), reads=[ob], writes=[Dq("X1B", g)], dma=True)

        st = {}
        for _ in tile3a_A(0, st):
            pass
        for g in range(8):
            ga = tile3a_A(g + 1, st) if g + 1 < 8 else iter(())
            interleave(ga, tile3a_B(g, *st[g]))

    hT = harena[:].rearrange("p (f t) -> p f t", f=NFC)

    def phase3b(l, last):
        wpg = wres[:].rearrange("p (k n) -> p k n", k=8)
        load_w(wpg, Wd_["wpg%d" % l].rearrange("(k p) n -> p k n", p=128), wres)
        load_w(wpp_t[:], Wd_["wpp%d" % l].rearrange("(k p) n -> p k n", p=128), wpp_t)
        WG, WU, WD = Wd_["wg%d" % l], Wd_["wu%d" % l], Wd_["wd%d" % l]

        def ffn_chunk(i, fc, f2, wg, wgb, wu, wub, xbh, xbh_v):
            cc = f32t.next()
            for sub in range(2):
                t0 = i * 512 + sub * 256
                G = ps_all.next()
                mm_group(G, 128, 258, [(wg[:, k, f2 * 128:(f2 + 1) * 128], xbh_v[:, k, t0:t0 + 258]) for k in range(8)], [wgb] + xbh.ch)
                csl = slice(sub * 256, (sub + 1) * 256)
                S.op(ACT, lambda e, G=G, csl=csl: e.activation(out=cc[:, csl], in_=G[:, 1:257], func=AF.Identity, scale=V_(l, 32 + 22, fc), bias=V_(l, 32 + 66, fc)), reads=[G, vec], accum=[cc])
                S.op(DVE, lambda e, G=G, csl=csl: e.scalar_tensor_tensor(out=cc[:, csl], in0=G[:, 0:256], scalar=V_(l, 32, fc), in1=cc[:, csl], op0=ALU.mult, op1=ALU.add), reads=[G, vec, cc], accum=[cc])
                S.op(DVE, lambda e, G=G, csl=csl: e.scalar_tensor_tensor(out=cc[:, csl], in0=G[:, 2:258], scalar=V_(l, 32 + 44, fc), in1=cc[:, csl], op0=ALU.mult, op1=ALU.add), reads=[G, vec, cc], accum=[cc])
            U = ps_all.next()
            mm_group(U, 128, 512, [(wu[:, k, f2 * 128:(f2 + 1) * 128], xbh_v[:, k, 1 + i * 512:1 + (i + 1) * 512]) for k in range(8)], [wub] + xbh.ch)
            gg = f32t.next()
            S.op(ACT, lambda e: e.activation(out=gg[:], in_=cc[:], func=AF.Gelu_apprx_tanh), reads=[cc], writes=[gg])
            S.op(DVE, lambda e: e.tensor_tensor(out=hT[:, fc, i * 512:(i + 1) * 512], in0=gg[:], in1=U[:], op=ALU.mult), reads=[gg, U], accum=[harena] + h_alias)

        def post_A2(blk, st):
            for i in range(2):
                g = blk * 2 + i
                gsl = slice(g * 512, (g + 1) * 512)
                B = bigs.next()
                xf8 = xf_view(B)
                st[i] = (B, xf8)
                S.op(SP, lambda e, xf8=xf8, gsl=gsl: e.dma_start(out=xf8, in_=XF[:, :, gsl].rearrange("c p t -> p c t")), reads=[Dq("XF", g)], writes=B.ch, dma=True)
            for dg in range(4):
                wdb = wring.next()
                wd = wdb[:, 0:5632].rearrange("p (f n) -> p f n", f=NFC)
                load_w(wd, WD[:, dg * 256:(dg + 1) * 256].rearrange("(f p) n -> p f n", p=128), wdb)
                for i in range(2):
                    B, xf8 = st[i]
                    for d2 in range(2):
                        c = dg * 2 + d2
                        P = ps_all.next()
                        mm_group(P, 128, 512, [(wd[:, f, d2 * 128:(d2 + 1) * 128], hT[:, f, i * 512:(i + 1) * 512]) for f in range(NFC)], [wdb, harena] + h_alias)
                        S.op(DVE, lambda e, c=c, P=P, xf8=xf8: e.scalar_tensor_tensor(out=xf8[:, c, :], in0=xf8[:, c, :], scalar=ALPHA, in1=P[:], op0=ALU.mult, op1=ALU.add), reads=[B.ch[c], P], writes=[B.ch[c]])

        def post_B(blk, i, big32, xf8):
            g = blk * 2 + i
            gsl = slice(g * 512, (g + 1) * 512)
            S.op(SP, lambda e: e.dma_start(out=ptile[:], in_=PT[l, :, :, gsl].rearrange("k p t -> p k t")), reads=[Dq("PT%d" % l, g)], writes=[ptile], dma=True)
            x2b = bigb.next()
            res = {}
            yield from layer_norm_gen(xf8, big32, l, 16, 24, x2b, x2b, res)
            x3b = res["zq"]
            for c in range(8):
                Pg = ps_all.next(); Pp = ps_all.next()
                mm_group(Pg, 128, 512, [(wpg[:, k, c * 128:(c + 1) * 128], x2b[:, k, :]) for k in range(8)], [wres, x2b])
                mm_group(Pp, 128, 512, [(wpp_t[:, k, c * 128:(c + 1) * 128], ptile[:, k, :]) for k in range(2)], [wpp_t, ptile])
                sg = f32t.next(); tt = f32t.next()
                S.op(ACT, lambda e, sg=sg, Pg=Pg: e.activation(out=sg[:], in_=Pg[:], func=AF.Sigmoid), reads=[Pg], writes=[sg])
                S.op(DVE, lambda e, sg=sg, tt=tt, Pp=Pp: e.tensor_tensor(out=tt[:], in0=sg[:], in1=Pp[:], op=ALU.mult), reads=[sg, Pp], writes=[tt])
                S.op(DVE, lambda e, c=c, tt=tt: e.tensor_tensor(out=xf8[:, c, :], in0=xf8[:, c, :], in1=tt[:], op=ALU.add), reads=[big32.ch[c], tt], writes=[big32.ch[c]])
                if not last:
                    S.op(ACT, lambda e, c=c: e.copy(out=x3b[:, c, :], in_=xf8[:, c, :]), reads=[big32.ch[c]], accum=[x3b])
            if not last:
                S.op(ACT, lambda e: e.dma_start(out=XF[:, :, gsl].rearrange("c p t -> p c t"), in_=xf8), reads=big32.ch, writes=[Dq("XF", g)], dma=True)
                S.op(ACT, lambda e: e.dma_start(out=XB[:, :, gsl].rearrange("c p t -> p c t"), in_=x3b[:]), reads=[x3b], writes=[Dq("XB", g)], dma=True)
            else:
                for j in range(4):
                    for hh in range(2):
                        P = ps_all.next()

                        def tro(e, P=P, j=j, hh=hh):
                            for c4 in range(4):
                                c = hh * 4 + c4
                                ins = e.transpose(out=P[:, c4 * 128:(c4 + 1) * 128], in_=xf8[:, c, j * 128:(j + 1) * 128], identity=ident[:])
                            return ins
                        S.op(PE, tro, reads=big32.ch + [ident], writes=[P])
                        f = f32t.next()
                        S.op(ACT, lambda e, f=f, P=P: e.copy(out=f[:], in_=P[:]), reads=[P], writes=[f])
                        r0 = g * 512 + j * 128
                        ev = S.op(SP, lambda e, f=f, r0=r0, hh=hh: e.dma_start(out=out_d[r0:r0 + 128, hh * 512:(hh + 1) * 512], in_=f[:]), reads=[f], dma=True)
                        S.out_evs.append(ev)

        def blk3b(blk):
            xbh = bigs.next()
            xbh_v = xbh_view(xbh)
            b0 = blk * 1024
            first = (blk % 2 == 0)
            lastb = (blk % 2 == 1)
            lo = b0 - (0 if first else 1)
            hi = b0 + 1024 + (0 if lastb else 1)
            dcol = 1 if first else 0
            rd = [Dq("X1B", g) for g in range(max(0, blk * 2 - 1), min(8, blk * 2 + 3))]
            S.op(SP, lambda e: e.dma_start(out=xbh_v[:, :, dcol:dcol + (hi - lo)], in_=X1B[:, :, lo:hi].rearrange("c p t -> p c t")), reads=rd, writes=xbh.ch, dma=True)
            if first:
                S.op(DVE, lambda e: e.memset(xbh_v[:, :, 0:1], 0.0), accum=xbh.ch)
            if lastb:
                S.op(DVE, lambda e: e.memset(xbh_v[:, :, 1025:1026], 0.0), accum=xbh.ch)
            for fg in range(11):
                wgb = wsm.next(); wub = wsm.next()
                wg = wgb[:].rearrange("p (k n) -> p k n", k=8)
                wu = wub[:].rearrange("p (k n) -> p k n", k=8)
                load_w(wg, WG[:, fg * 256:(fg + 1) * 256].rearrange("(k p) n -> p k n", p=128), wgb)
                load_w(wu, WU[:, fg * 256:(fg + 1) * 256].rearrange("(k p) n -> p k n", p=128), wub)
                for i in range(2):
                    for f2 in range(2):
                        ffn_chunk(i, fg * 2 + f2, f2, wg, wgb, wu, wub, xbh, xbh_v)
            st = {}
            post_A2(blk, st)
            for i in range(2):
                for _ in post_B(blk, i, *st[i]):
                    pass

        for blk in range(4):
            blk3b(blk)

    stages = [("p1_0", phase1_layer0), ("attA", lambda: attention("A")), ("attB", lambda: attention("B")),
              ("p3a_0", lambda: phase3a(0)), ("p3b_0", lambda: phase3b(0, False)),
              ("p1_1", phase1_layer1), ("attC", lambda: attention("C")),
              ("p3a_1", lambda: phase3a(1)), ("p3b_1", lambda: phase3b(1, True))]
    for nm, fn in stages:
        if stop_after == "none":
            break
        fn()
        if stop_after == nm:
            break
    if not S.out_evs:
        for e_ in ("pe", "act", "dve", "pool"):
            if S.cnt[e_]:
                S.out_evs.append(("e" + e_, S.cnt[e_]))
        for k, v in S.dma_cnt.items():
            S.out_evs.append((k, v))
    S.emit(es)
    es.close()
    return nc


_CONSTS = None


def kernel(**inputs):
    global _CONSTS
    inp = {k: np.asarray(v) for k, v in inputs.items()}
    w = pack_weights(inp)
    if _CONSTS is None:
        _CONSTS = make_consts()
    w.update(_CONSTS)
    x = np.asarray(inp["x"], np.float32)
    p = np.asarray(inp["p"], np.float32)
    pos = np.asarray(inp["positions"], np.int32)
    nc = build_program()
    in_maps = []
    for c in range(8):
        m = dict(w)
        m["x"] = np.ascontiguousarray(x[2 * c:2 * c + 2].reshape(NT, D))
        m["p"] = np.ascontiguousarray(p[:, 2 * c:2 * c + 2].reshape(DEPTH, NT, 256))
        m["pos"] = np.ascontiguousarray(pos[2 * c:2 * c + 2])
        in_maps.append(m)
    res = run_bass_kernel_spmd(nc, in_maps, core_ids=list(range(8)))
    out = np.concatenate([np.asarray(r["out"], np.float32).reshape(2, T, D) for r in res.results], axis=0)
    return out
```

```python
import math
from contextlib import ExitStack

import numpy as np
import concourse.bass as bass
import concourse.mybir as mybir
from concourse.bass_utils import run_bass_kernel_spmd

F32 = mybir.dt.float32
BF16 = mybir.dt.bfloat16
I32 = mybir.dt.int32
AF = mybir.ActivationFunctionType
ALU = mybir.AluOpType

D = 1024
T = 2048
NSEQ = 2
NT = NSEQ * T
DFF = 2816
NFC = DFF // 128
DEPTH = 2
ALPHA = float((2 * DEPTH) ** 0.25)
LN_EPS = 1e-5
RMS_EPS = 1e-6
ROPE_THETA = 500000.0
TWO_PI = float(2 * np.pi)
MASK_U0 = 1920
MASK_W = 3968


class Buf:
    def __init__(self, t=None, name=""):
        self.t = t
        self.name = name
        self.w = {}
        self.r = {}
        self.g = {}

    def __getitem__(self, idx):
        return self.t[idx]


class Ring:
    def __init__(self, bufs):
        self.bufs = bufs
        self.i = 0

    def next(self):
        b = self.bufs[self.i]
        self.i = (self.i + 1) % len(self.bufs)
        return b


class Sched:
    ENGS = ("pe", "act", "dve", "pool", "sp")

    def __init__(self, nc, n_dma_sems=14):
        self.nc = nc
        self.ops = {e: [] for e in self.ENGS}
        self.cnt = {e: 0 for e in self.ENGS}
        self.waited = {e: {} for e in self.ENGS}
        self.n_dma = n_dma_sems
        self.dma_cnt = {}
        self.dma_rr = {"sp": 0, "pool": 0, "act": 0}
        self.sems = {}
        self.out_evs = []

    def op(self, eng, fn, reads=(), writes=(), accum=(), dma=False):
        waits = {}
        own = "e" + eng

        def need(k, v, raw):
            if k == own and eng == "pe":
                return
            if v > waits.get(k, 0):
                waits[k] = v

        for b in reads:
            for k, v in b.w.items():
                need(k, v, True)
        for b in writes:
            for k, v in b.w.items():
                need(k, v, False)
            for k, v in b.r.items():
                need(k, v, False)
        for b in accum:
            for k, v in b.r.items():
                need(k, v, False)
            if not b.r:
                for k, v in b.g.items():
                    need(k, v, False)
        if dma:
            i = self.dma_rr[eng]
            self.dma_rr[eng] = (i + 1) % self.n_dma
            key = "d%s%d" % (eng, i)
            prev = self.dma_cnt.get(key, 0)
            if prev > 0 and prev > waits.get(key, 0):
                waits[key] = prev
            val = prev + 16
            self.dma_cnt[key] = val
        else:
            self.cnt[eng] += 1
            key, val = own, self.cnt[eng]
        wl = []
        wd = self.waited[eng]
        for k, v in waits.items():
            if wd.get(k, 0) >= v:
                continue
            wd[k] = v
            wl.append((k, v))
        for b in reads:
            if b.r.get(key, 0) < val:
                b.r[key] = val
        for b in writes:
            g = dict(b.w)
            for k, v in b.r.items():
                if g.get(k, 0) < v:
                    g[k] = v
            b.g = g
            b.w = {key: val}
            b.r = {}
        for b in accum:
            if b.r:
                b.g = dict(b.r)
                b.w = {key: val}
                b.r = {}
            elif b.w.get(key, 0) < val:
                b.w[key] = val
        self.ops[eng].append((wl, fn, key))
        return (key, val)

    def emit(self, es):
        nc = self.nc
        self.ops["sp"].append((list(self.out_evs), None, None))
        keys = set()
        for e in self.ENGS:
            for wl, fn, key in self.ops[e]:
                if key is not None:
                    keys.add(key)
                for k, v in wl:
                    keys.add(k)
        for k in sorted(keys):
            self.sems[k] = es.enter_context(nc.semaphore(k))
        block = es.enter_context(nc.Block())
        sems = self.sems

        def run(engname):
            def body(eng):
                for wl, fn, key in self.ops[engname]:
                    for k, v in wl:
                        eng.wait_ge(sems[k], v)
                    if fn is None:
                        continue
                    ins = fn(eng)
                    ins.then_inc(sems[key], 16 if key[0] == "d" else 1)
            return body

        block.tensor(run("pe"))
        block.scalar(run("act"))
        block.vector(run("dve"))
        block.gpsimd(run("pool"))
        block.sync(run("sp"))


def _pair_partner_cols(base, nheads, hd, rot):
    half = rot // 2
    idx = []
    for h in range(nheads):
        b = base + h * hd
        idx += [b + half + i for i in range(half)] + [b + i for i in range(half)] + [b + i for i in range(rot, hd)]
    return np.array(idx)


def _interleave_pairs(W, base, npairs):
    cols = []
    for j in range(npairs):
        main = np.arange(base + j * 128, base + (j + 1) * 128)
        part = _pair_partner_cols(base + j * 128, 2, 64, 16)
        cols += [main, part]
    return np.concatenate(cols)


def pack_weights(inp):
    w = {}
    w_in = inp["ab_w_in"][0]
    cq0, ckv0, kpe0 = 1536, 1920, 2176
    kpe = np.arange(kpe0, kpe0 + 32)
    kpe_p = np.concatenate([kpe[16:], kpe[:16]])
    dummy = np.arange(0, 64)
    cols0 = np.concatenate([
        np.arange(0, 1024),
        np.arange(cq0, cq0 + 384), np.arange(ckv0, ckv0 + 256),
        dummy, kpe, dummy, kpe_p])
    w["w0p"] = np.ascontiguousarray(w_in[:, cols0])
    w["w0v"] = np.ascontiguousarray(w_in[:, 1024:1536])
    wq = inp["ab_w_q_up"][0]
    qcols = []
    for h in range(8):
        b = h * 96
        main = np.arange(b, b + 96)
        part = np.concatenate([np.arange(b, b + 64), np.arange(b + 80, b + 96), np.arange(b + 64, b + 80)])
        qcols += [main, part]
    w["wqu"] = np.ascontiguousarray(wq[:, np.concatenate(qcols)])
    wkv = inp["ab_w_kv_up"][0]
    kn = np.concatenate([np.arange(h * 128, h * 128 + 64) for h in range(8)])
    vb = np.concatenate([np.arange(h * 128 + 64, h * 128 + 128) for h in range(8)])
    w["wkn"] = np.ascontiguousarray(wkv[:, kn])
    w["wvb"] = np.ascontiguousarray(wkv[:, vb])
    w["wo0"] = np.ascontiguousarray(inp["ab_w_out"][0])
    wc = inp["c_w_qkv"][0]
    w["w1p"] = np.ascontiguousarray(wc[:, 0:2048])
    w["w1v"] = np.ascontiguousarray(wc[:, 2048:3072])
    w["wo1"] = np.ascontiguousarray(inp["c_w_out"][0])
    for l in range(DEPTH):
        w["wg%d" % l] = np.ascontiguousarray(inp["ffn_w_gate"][l])
        w["wu%d" % l] = np.ascontiguousarray(inp["ffn_w_up"][l])
        w["wd%d" % l] = np.ascontiguousarray(inp["ffn_w_down"][l])
        w["wpg%d" % l] = np.ascontiguousarray(inp["ple_w_gate"][l])
        w["wpp%d" % l] = np.ascontiguousarray(inp["ple_w_proj"][l])
    cols = []

    def pp(v):
        v = np.asarray(v, np.float32)
        cols.append(v.reshape(-1, 128).T)

    for l in range(DEPTH):
        pp(inp["ln_mix_g"][l]); pp(inp["ln_mix_b"][l]); pp(inp["ln_ffn_g"][l]); pp(inp["ln_ffn_b"][l])
        pp(inp["ffn_conv_w"][l][0]); pp(inp["ffn_conv_w"][l][1]); pp(inp["ffn_conv_w"][l][2]); pp(inp["ffn_conv_b"][l])
    pp(inp["ab_q_norm"][0]); pp(inp["ab_kv_norm"][0]); pp(inp["c_subln"][0])
    w["vec"] = np.ascontiguousarray(np.concatenate(cols, axis=1))
    w["lam"] = np.ascontiguousarray(np.asarray(inp["c_lambda"][0], np.float32).reshape(1, 256))
    return w


VEC_L = 120
V_QN = 2 * VEC_L
V_KVN = V_QN + 3
V_SUB = V_KVN + 2
NVEC = V_SUB + 1


def make_consts():
    c = {}
    c["ident"] = np.eye(128, dtype=np.float32)
    fa = (1.0 / (ROPE_THETA ** (np.arange(0, 16, 2, dtype=np.float32) / np.float32(16)))).astype(np.float32)
    fb = (1.0 / (ROPE_THETA ** (np.arange(0, 32, 2, dtype=np.float32) / np.float32(32)))).astype(np.float32)
    rc = np.zeros((128, 8), np.float32)
    for p in range(128):
        i = p % 64
        if i < 16:
            rc[p, 0] = fa[i % 8] / TWO_PI
            rc[p, 1] = -TWO_PI if i < 8 else TWO_PI
        if 64 <= p < 96:
            j = p - 64
            rc[p, 2] = fb[j % 16] / TWO_PI
            rc[p, 3] = -TWO_PI if j < 16 else TWO_PI
    rc[:, 4] = TWO_PI
    rc[:, 5] = LN_EPS
    rc[:, 6] = RMS_EPS
    c["rc"] = rc
    p = np.arange(128)[:, None]
    u = np.arange(MASK_W)[None, :]
    off = p - u + MASK_U0
    a = np.abs(off)
    m = (a <= 64).astype(np.float32) + ((off % 4 == 0) & (a <= 256)) + ((off % 16 == 0) & (a <= 1024))
    c["maskA"] = np.ascontiguousarray(m.astype(np.float32))
    rp = np.zeros((128, 128), np.float32)
    for m_ in range(128):
        i = m_ % 64
        k_ = m_ + 8 if i < 8 else (m_ - 8 if i < 16 else m_)
        rp[k_, m_] = 1.0
    c["rperm"] = rp
    return c


_MASK_NP = None


def mask_np():
    global _MASK_NP
    if _MASK_NP is None:
        _MASK_NP = make_consts()["maskA"]
    return _MASK_NP


def build_program(dbg=False, stop_after=None):
    nc = bass.Bass("TRN2", target_bir_lowering=False)

    def din(name, shape, dt=F32):
        return nc.dram_tensor(name, list(shape), dt, kind="ExternalInput").ap()

    def dscr(name, shape, dt):
        return nc.dram_tensor(name, list(shape), dt, kind="ExternalOutput" if dbg else "Internal").ap()

    x_in = din("x", [NT, D])
    p_in = din("p", [DEPTH, NT, 256])
    pos_in = din("pos", [NSEQ, T], I32)
    Wd_ = {}
    for name, shape in [("w0p", [D, 1856]), ("w0v", [D, 512]), ("wqu", [384, 1536]), ("wkn", [256, 512]),
                        ("wvb", [256, 512]), ("wo0", [D, D]), ("w1p", [D, 2048]), ("w1v", [D, 1024]),
                        ("wo1", [D, D]), ("vec", [128, NVEC]), ("lam", [1, 256]),
                        ("ident", [128, 128]), ("rperm", [128, 128]), ("rc", [128, 8]), ("maskA", [128, MASK_W])]:
        Wd_[name] = din(name, shape)
    for l in range(DEPTH):
        Wd_["wg%d" % l] = din("wg%d" % l, [D, DFF])
        Wd_["wu%d" % l] = din("wu%d" % l, [D, DFF])
        Wd_["wd%d" % l] = din("wd%d" % l, [DFF, D])
        Wd_["wpg%d" % l] = din("wpg%d" % l, [D, D])
        Wd_["wpp%d" % l] = din("wpp%d" % l, [256, D])
    out_d = nc.dram_tensor("out", [NT, D], F32, kind="ExternalOutput").ap()

    XF = dscr("XF", [8, 128, NT], F32)
    XB = dscr("XB", [8, 128, NT], BF16)
    X1B = dscr("X1B", [8, 128, NT], BF16)
    TAB = dscr("TAB", [4, 128, NT], F32)
    PT = dscr("PT", [DEPTH, 2, 128, NT], BF16)
    QA = dscr("QA", [4, 128, NT], BF16)
    KA = dscr("KA", [4, 128, NT], BF16)
    VA = dscr("VA", [NT, 512], BF16)
    QB = dscr("QB", [8, 96, NT], BF16)
    KB = dscr("KB", [8, 96, NT], BF16)
    VB = dscr("VB", [NT, 512], BF16)
    QC = dscr("QC", [8, 128, NT], BF16)
    KC = dscr("KC", [8, 128, NT], BF16)
    VC = dscr("VC", [NT, 1024], BF16)
    AO = dscr("AO", [8, 128, NT], BF16)

    es = ExitStack()
    S = Sched(nc)
    dbufs = {}

    def Dq(name, i):
        k = (name, i)
        if k not in dbufs:
            dbufs[k] = Buf(None, "%s%d" % (name, i))
        return dbufs[k]

    def sb(name, shape, dt):
        return Buf(es.enter_context(nc.sbuf_tensor("s_" + name, list(shape), dt)), name)

    PSB = [Buf(es.enter_context(nc.psum_tensor("ps%d" % i, [128, 512], F32)), "ps%d" % i) for i in range(8)]
    ps_all = Ring(PSB)

    ident = sb("ident", [128, 128], F32)
    ones = sb("ones", [128, 128], BF16)
    rperm = sb("rperm", [128, 128], BF16)
    vec = sb("vec", [128, NVEC], F32)
    rc = sb("rc", [128, 8], F32)
    lamb = sb("lamb", [128, 256], F32)
    lamt = sb("lamt", [128, 8], F32)
    gsub = sb("gsub", [128, 1], F32)
    f32t = Ring([sb("f32t%d" % i, [128, 512], F32) for i in range(8)])
    b16t = Ring([sb("b16t%d" % i, [128, 512], BF16) for i in range(8)])
    big32 = sb("big32", [128, 4104], F32)
    big32b = sb("big32b", [128, 4104], F32)
    for B_ in (big32, big32b):
        B_.ch = [Buf(B_.t, "%s_c%d" % (B_.name, c)) for c in range(8)]
    bigs = Ring([big32, big32b])

    def xf_view(B):
        return B[:, 0:4096].rearrange("p (c t) -> p c t", c=8)

    def xbh_view(B):
        return B[:, 0:4104].bitcast(BF16).rearrange("p (c t) -> p c t", c=8)
    bigb = Ring([sb("bigb%d" % i, [128, 8, 512], BF16) for i in range(3)])
    wring = Ring([sb("wring%d" % i, [128, 5632], BF16) for i in range(2)])
    wres = sb("wres", [128, 8192], BF16)
    wpp_t = sb("wpp", [128, 2, 1024], BF16)
    harena = sb("harena", [128, 22528], BF16)
    tabs = [sb("tab%d" % i, [128, 512], F32) for i in range(4)]
    wsm = Ring([sb("wsm%d" % i, [128, 2048], BF16) for i in range(6)])
    ptile = sb("ptile", [128, 2, 512], BF16)
    att = [Buf(harena.t, "att%d" % i) for i in range(6)]
    h_alias = att

    def V_(l, off, c):
        return vec[:, l * VEC_L + off + c: l * VEC_L + off + c + 1]

    ACT, DVE, POOL, PE, SP = "act", "dve", "pool", "pe", "sp"

    S.op(SP, lambda e: e.dma_start(out=ident[:], in_=Wd_["ident"]), writes=[ident], dma=True)
    S.op(SP, lambda e: e.dma_start(out=vec[:], in_=Wd_["vec"]), writes=[vec], dma=True)
    S.op(SP, lambda e: e.dma_start(out=rc[:], in_=Wd_["rc"]), writes=[rc], dma=True)
    S.op(SP, lambda e: e.dma_start(out=lamb[:], in_=Wd_["lam"].partition_broadcast(128)), writes=[lamb], dma=True)
    S.op(DVE, lambda e: e.memset(ones[:], 1.0), writes=[ones])
    S.op(POOL, lambda e: e.dma_start(out=rperm[:], in_=Wd_["rperm"]), writes=[rperm], dma=True)
    lambda_init = 0.8 - 0.6 * math.exp(-0.3 * 1)
    t_ = f32t.next()
    S.op(DVE, lambda e: e.tensor_tensor(out=t_[:, 0:64], in0=lamb[:, 0:64], in1=lamb[:, 64:128], op=ALU.mult), reads=[lamb], writes=[t_])
    S.op(DVE, lambda e: e.tensor_tensor(out=t_[:, 64:128], in0=lamb[:, 128:192], in1=lamb[:, 192:256], op=ALU.mult), reads=[lamb], accum=[t_])
    S.op(DVE, lambda e: e.reduce_sum(out=lamt[:, 0:1], in_=t_[:, 0:64], axis=mybir.AxisListType.X), reads=[t_], writes=[lamt])
    S.op(DVE, lambda e: e.reduce_sum(out=lamt[:, 1:2], in_=t_[:, 64:128], axis=mybir.AxisListType.X), reads=[t_], accum=[lamt])
    S.op(ACT, lambda e: e.activation(out=lamt[:, 2:4], in_=lamt[:, 0:2], func=AF.Exp), reads=[lamt], accum=[lamt])
    S.op(DVE, lambda e: e.tensor_tensor(out=lamt[:, 4:5], in0=lamt[:, 3:4], in1=lamt[:, 2:3], op=ALU.subtract), reads=[lamt], accum=[lamt])
    S.op(DVE, lambda e: e.tensor_scalar(out=lamt[:, 5:6], in0=lamt[:, 4:5], scalar1=float(-lambda_init), scalar2=None, op0=ALU.add), reads=[lamt], accum=[lamt])
    nlam = lamt[:, 5:6]
    S.op(DVE, lambda e: e.tensor_scalar(out=gsub[:], in0=vec[:, V_SUB:V_SUB + 1], scalar1=float(1.0 - lambda_init), scalar2=None, op0=ALU.mult), reads=[vec], writes=[gsub])

    import os as _os
    _lim = _os.environ.get("K_LIM", "")
    posf = big32[:, 0:2048]
    posi = big32[:, 2048:4096].bitcast(I32)
    for s in range(NSEQ if _lim != "setup" else 0):
        S.op(SP, lambda e, s=s: e.dma_start(out=posi, in_=pos_in[s:s + 1, :].partition_broadcast(128)), writes=big32.ch, dma=True)
        S.op(DVE, lambda e: e.tensor_copy(out=posf, in_=posi), reads=big32.ch, accum=big32.ch)
        for kind in range(4):
            fcol = 0 if kind < 2 else 2
            scol = 4 if kind % 2 == 0 else (1 if kind == 1 else 3)
            ph = 0.25 if kind % 2 == 0 else 0.0
            for tl in range(4):
                a = f32t.next(); b = f32t.next(); c = f32t.next()
                sl = slice(tl * 512, (tl + 1) * 512)
                S.op(DVE, lambda e, a=a, sl=sl, fcol=fcol, ph=ph: e.tensor_scalar(out=a[:], in0=posf[:, sl], scalar1=rc[:, fcol:fcol + 1], scalar2=float(ph), op0=ALU.mult, op1=ALU.add), reads=big32.ch + [rc], writes=[a])
                S.op(DVE, lambda e, a=a, b=b: e.tensor_copy(out=b[:].bitcast(I32), in_=a[:]), reads=[a], writes=[b])
                S.op(DVE, lambda e, b=b, c=c: e.tensor_copy(out=c[:], in_=b[:].bitcast(I32)), reads=[b], writes=[c])
                S.op(DVE, lambda e, a=a, c=c: e.tensor_tensor(out=a[:], in0=a[:], in1=c[:], op=ALU.subtract), reads=[a, c], writes=[a])
                S.op(DVE, lambda e, a=a, c=c: e.tensor_scalar(out=c[:], in0=a[:], scalar1=0.5, scalar2=None, op0=ALU.is_gt), reads=[a], writes=[c])
                S.op(DVE, lambda e, a=a, c=c: e.tensor_tensor(out=a[:], in0=a[:], in1=c[:], op=ALU.subtract), reads=[a, c], writes=[a])
                S.op(DVE, lambda e, a=a, c=c: e.tensor_scalar(out=c[:], in0=a[:], scalar1=-0.5, scalar2=None, op0=ALU.is_lt), reads=[a], writes=[c])
                S.op(DVE, lambda e, a=a, c=c: e.tensor_tensor(out=a[:], in0=a[:], in1=c[:], op=ALU.add), reads=[a, c], writes=[a])
                S.op(ACT, lambda e, a=a, b=b, scol=scol: e.activation(out=b[:], in_=a[:], func=AF.Sin, scale=rc[:, scol:scol + 1]), reads=[a, rc], writes=[b])
                g = s * 4 + tl
                S.op(SP, lambda e, b=b, kind=kind, g=g: e.dma_start(out=TAB[kind, :, g * 512:(g + 1) * 512], in_=b[:]), reads=[b], accum=[Dq("TAB", g)], dma=True)

    for g in range(int(_os.environ.get("K_XG", "8")) if _lim not in ("setup", "tabs") else 0):
        BX = bigs.next()
        xin4 = BX[:, 0:4096].rearrange("p (j d) -> p j d", j=4)
        S.op(SP, lambda e, g=g, xin4=xin4: e.dma_start(out=xin4, in_=x_in[g * 512:(g + 1) * 512, :].rearrange("(j p) d -> p j d", p=128)), writes=BX.ch, dma=True)
        for c in range(8):
            P = ps_all.next()

            def tr(e, P=P, c=c, xin4=xin4):
                for j in range(4):
                    ins = e.transpose(out=P[:, j * 128:(j + 1) * 128], in_=xin4[:, j, c * 128:(c + 1) * 128], identity=ident[:])
                return ins
            _xs = int(_os.environ.get("K_XS", "3"))
            if _xs >= 1:
                S.op(PE, tr, reads=BX.ch + [ident], writes=[P])
            f = f32t.next(); bb = b16t.next()
            if _xs >= 2:
                _xc = _os.environ.get("K_XC", "both")
                if _xc in ("both", "act"):
                    S.op(ACT, lambda e, f=f, P=P: e.copy(out=f[:], in_=P[:]), reads=[P], writes=[f])
                if _xc in ("both", "dve"):
                    S.op(DVE, lambda e, bb=bb, f=f: e.tensor_copy(out=bb[:], in_=f[:]), reads=[f], writes=[bb])
            if _xs >= 3:
                S.op(ACT, lambda e, f=f, c=c, g=g: e.dma_start(out=XF[c, :, g * 512:(g + 1) * 512], in_=f[:]), reads=[f], accum=[Dq("XF", g)], dma=True)
                S.op(ACT, lambda e, bb=bb, c=c, g=g: e.dma_start(out=XB[c, :, g * 512:(g + 1) * 512], in_=bb[:]), reads=[bb], accum=[Dq("XB", g)], dma=True)
        for l in range(DEPTH if _os.environ.get("K_XP", "1") == "1" else 0):
            BP = bigs.next()
            pin4 = BP[:, 0:1024].rearrange("p (j d) -> p j d", j=4)
            S.op(SP, lambda e, g=g, l=l, pin4=pin4: e.dma_start(out=pin4, in_=p_in[l, g * 512:(g + 1) * 512, :].rearrange("(j p) d -> p j d", p=128)), writes=BP.ch, dma=True)
            for k in range(2):
                P = ps_all.next()

                def trp(e, P=P, k=k, pin4=pin4):
                    for j in range(4):
                        ins = e.transpose(out=P[:, j * 128:(j + 1) * 128], in_=pin4[:, j, k * 128:(k + 1) * 128], identity=ident[:])
                    return ins
                S.op(PE, trp, reads=BP.ch + [ident], writes=[P])
                bb = b16t.next()
                S.op(ACT, lambda e, bb=bb, P=P: e.copy(out=bb[:], in_=P[:]), reads=[P], writes=[bb])
                S.op(ACT, lambda e, bb=bb, l=l, k=k, g=g: e.dma_start(out=PT[l, k, :, g * 512:(g + 1) * 512], in_=bb[:]), reads=[bb], accum=[Dq("PT%d" % l, g)], dma=True)

    def load_w(dst_ap, src_ap, buf):
        S.op(POOL, lambda e: e.dma_start(out=dst_ap, in_=src_ap), writes=[buf], dma=True)

    def mm_group(P, M, N, pairs, extra_reads, pcol0=0):
        def f(e):
            n = len(pairs)
            for i, (l, r) in enumerate(pairs):
                ins = e.matmul(P[0:M, pcol0:pcol0 + N], lhsT=l, rhs=r, start=(i == 0), stop=(i == n - 1))
            return ins
        S.op(PE, f, reads=extra_reads, writes=[P])

    def rope_store(P1, P2, M, Ct, St, dst_ap, dst_buf, p0=0):
        t1 = f32t.next(); t2 = f32t.next(); o = b16t.next()
        S.op(DVE, lambda e: e.tensor_tensor(out=t1[p0:p0 + M, :], in0=P1[p0:p0 + M, :], in1=Ct[p0:p0 + M, :], op=ALU.mult), reads=[P1, Ct], writes=[t1])
        S.op(DVE, lambda e: e.tensor_tensor(out=t2[p0:p0 + M, :], in0=P2[p0:p0 + M, :], in1=St[p0:p0 + M, :], op=ALU.mult), reads=[P2, St], writes=[t2])
        S.op(DVE, lambda e: e.tensor_tensor(out=o[p0:p0 + M, :], in0=t1[p0:p0 + M, :], in1=t2[p0:p0 + M, :], op=ALU.add), reads=[t1, t2], writes=[o])
        if isinstance(dst_ap, list):
            for d in dst_ap:
                S.op(SP, lambda e, d=d: e.dma_start(out=d, in_=o[p0:p0 + M, :]), reads=[o], accum=[dst_buf], dma=True)
        else:
            S.op(SP, lambda e: e.dma_start(out=dst_ap, in_=o[p0:p0 + M, :]), reads=[o], accum=[dst_buf], dma=True)

    def rope_store_r(P1, Ct, St, dst_ap, dst_buf):
        qb_ = b16t.next()
        S.op(ACT, lambda e: e.copy(out=qb_[:], in_=P1[:]), reads=[P1], writes=[qb_])
        P2 = ps_all.next()
        mm_group(P2, 128, 512, [(rperm[:], qb_[:])], [rperm, qb_])
        t1 = f32t.next(); t2 = f32t.next(); o = b16t.next()
        S.op(DVE, lambda e: e.tensor_tensor(out=t1[:], in0=P1[:], in1=Ct[:], op=ALU.mult), reads=[P1, Ct, qb_], writes=[t1])
        S.op(DVE, lambda e: e.tensor_tensor(out=t2[:], in0=P2[:], in1=St[:], op=ALU.mult), reads=[P2, St], writes=[t2])
        S.op(DVE, lambda e: e.tensor_tensor(out=o[:], in0=t1[:], in1=t2[:], op=ALU.add), reads=[t1, t2], writes=[o])
        S.op(SP, lambda e: e.dma_start(out=dst_ap, in_=o[:]), reads=[o], accum=[dst_buf], dma=True)

    def load_tabs(g, kinds, slot):
        for n_, k in enumerate(kinds):
            S.op(SP, lambda e, k=k, n_=n_: e.dma_start(out=tabs[2 * slot + n_][:], in_=TAB[k, :, g * 512:(g + 1) * 512]), reads=[Dq("TAB", g)], writes=[tabs[2 * slot + n_]], dma=True)
        return tabs[2 * slot], tabs[2 * slot + 1]

    def rstd_from_ss(Pss, scale, epscol):
        r = f32t.next()
        S.op(ACT, lambda e: e.activation(out=r[:], in_=Pss[:], func=AF.Ln, bias=rc[:, epscol:epscol + 1], scale=float(scale)), reads=[Pss, rc], writes=[r])
        S.op(ACT, lambda e: e.activation(out=r[:], in_=r[:], func=AF.Exp, scale=-0.5), reads=[r], writes=[r])
        return r

    def phase1_layer0():
        W = Wd_["w0p"]
        wqu = wres[:, 0:4608].rearrange("p (k n) -> p k n", k=3)
        wkn = wres[:, 4608:5632].rearrange("p (k n) -> p k n", k=2)
        wvb = wres[:, 5632:6656].rearrange("p (k n) -> p k n", k=2)
        load_w(wqu, Wd_["wqu"].rearrange("(k p) n -> p k n", p=128), wres)
        S.op(POOL, lambda e: e.dma_start(out=wkn, in_=Wd_["wkn"].rearrange("(k p) n -> p k n", p=128)), accum=[wres], dma=True)
        S.op(POOL, lambda e: e.dma_start(out=wvb, in_=Wd_["wvb"].rearrange("(k p) n -> p k n", p=128)), accum=[wres], dma=True)

        def load_x(g):
            x = bigb.next()
            S.op(SP, lambda e: e.dma_start(out=x[:], in_=XB[:, :, g * 512:(g + 1) * 512].rearrange("c p t -> p c t")), reads=[Dq("XB", g)], writes=[x], dma=True)
            return x

        def proj(wv, wb, x, col, M):
            P = ps_all.next()
            mm_group(P, M, 512, [(wv[:, k, col:col + M], x[:, k, :]) for k in range(8)], [wb, x])
            return P

        def evac_f32(P):
            f = f32t.next()
            S.op(ACT, lambda e: e.copy(out=f[:], in_=P[:]), reads=[P], writes=[f])
            return f

        def store_copy(P, M, dst_ap, dbuf):
            o = b16t.next()
            S.op(ACT, lambda e: e.copy(out=o[0:M, :], in_=P[0:M, :]), reads=[P], writes=[o])
            S.op(SP, lambda e: e.dma_start(out=dst_ap, in_=o[0:M, :]), reads=[o], accum=[dbuf], dma=True)

        def latent_tile(g, x, wv4, wb4, wv5, wb5):
            tsl = slice(g * 512, (g + 1) * 512)
            lat = bigb.next()
            raw = [evac_f32(proj(wv4, wb4, x, cc * 128, 128)) for cc in range(4)]
            raw.append(evac_f32(proj(wv5, wb5, x, 0, 128)))
            for (idxs, vcol, nfeat) in (((0, 1, 2), V_QN, 384), ((3, 4), V_KVN, 256)):
                sq = []
                for ci in idxs:
                    q = b16t.next()
                    S.op(ACT, lambda e, q=q, f=raw[ci]: e.activation(out=q[:], in_=f[:], func=AF.Square), reads=[raw[ci]], writes=[q])
                    sq.append(q)
                Pss = ps_all.next()
                mm_group(Pss, 128, 512, [(ones[:], q[:]) for q in sq], [ones] + sq)
                r = rstd_from_ss(Pss, 1.0 / nfeat, 6)
                for n_, ci in enumerate(idxs):
                    S.op(DVE, lambda e, ci=ci, n_=n_, vcol=vcol, r=r: e.scalar_tensor_tensor(out=lat[:, ci, :], in0=raw[ci][:], scalar=vec[:, vcol + n_:vcol + n_ + 1], in1=r[:], op0=ALU.mult, op1=ALU.mult), reads=[raw[ci], r, vec], accum=[lat])
            tC, tS = load_tabs(g, [2, 3], g % 2)
            P1 = proj(wv5, wb5, x, 128, 96)
            P2 = proj(wv5, wb5, x, 224, 96)
            rope_store(P1, P2, 32, tC, tS, [KB[h, 64:96, tsl] for h in range(8)], Dq("KB", g), p0=64)
            for h in range(8):
                P1 = ps_all.next(); P2 = ps_all.next()
                mm_group(P1, 96, 512, [(wqu[:, k, h * 192:h * 192 + 96], lat[:, k, :]) for k in range(3)], [wres, lat])
                mm_group(P2, 96, 512, [(wqu[:, k, h * 192 + 96:h * 192 + 192], lat[:, k, :]) for k in range(3)], [wres, lat])
                rope_store(P1, P2, 96, tC, tS, QB[h, :, tsl], Dq("QB", g))
            for h in range(8):
                P = ps_all.next()
                mm_group(P, 64, 512, [(wkn[:, k, h * 64:(h + 1) * 64], lat[:, 3 + k, :]) for k in range(2)], [wres, lat])
                store_copy(P, 64, KB[h, 0:64, tsl], Dq("KB", g))
            for j in range(4):
                P = ps_all.next()
                mm_group(P, 128, 512, [(lat[:, 3 + k, j * 128:(j + 1) * 128], wvb[:, k, :]) for k in range(2)], [wres, lat])
                store_copy(P, 128, VB[g * 512 + j * 128:g * 512 + (j + 1) * 128, :], Dq("VB", g))

        for blk in range(4):
            xt = [load_x(blk * 2 + i) for i in range(2)]
            tA = [load_tabs(blk * 2 + i, [0, 1], i) for i in range(2)]
            pend = []
            for gi in range(4):
                wb = wsm.next()
                wv = wb[:].rearrange("p (k n) -> p k n", k=8)
                load_w(wv, W[:, gi * 256:(gi + 1) * 256].rearrange("(k p) n -> p k n", p=128), wb)
                for i in range(2):
                    g = blk * 2 + i
                    tsl = slice(g * 512, (g + 1) * 512)
                    dst = QA if gi < 2 else KA
                    nm = "QA" if gi < 2 else "KA"
                    for jj in range(2):
                        j = (gi % 2) * 2 + jj
                        P1 = proj(wv, wb, xt[i], jj * 128, 128)
                        if pend:
                            rope_store_r(*pend.pop())
                        pend.append((P1, tA[i][0], tA[i][1], dst[j, :, tsl], Dq(nm, g)))
            if pend:
                rope_store_r(*pend.pop())
            wb = wring.next()
            wv = wb[:, 0:4096].rearrange("p (k n) -> p k n", k=8)
            load_w(wv, Wd_["w0v"].rearrange("(k p) n -> p k n", p=128), wb)
            for i in range(2):
                g = blk * 2 + i
                for j in range(4):
                    P = ps_all.next()
                    mm_group(P, 128, 512, [(xt[i][:, k, j * 128:(j + 1) * 128], wv[:, k, :]) for k in range(8)], [wb, xt[i]])
                    store_copy(P, 128, VA[g * 512 + j * 128:g * 512 + (j + 1) * 128, :], Dq("VA", g))
            wb4 = wring.next(); wb5 = wring.next()
            wv4 = wb4[:, 0:4096].rearrange("p (k n) -> p k n", k=8)
            wv5 = wb5[:, 0:2560].rearrange("p (k n) -> p k n", k=8)
            load_w(wv4, W[:, 1024:1536].rearrange("(k p) n -> p k n", p=128), wb4)
            load_w(wv5, W[:, 1536:1856].rearrange("(k p) n -> p k n", p=128), wb5)
            for i in range(2):
                latent_tile(blk * 2 + i, xt[i], wv4, wb4, wv5, wb5)

    lat_state = {}

    def phase1_layer1():
        W = Wd_["w1p"]
        for blk in range(4):
            xt = [bigb.next(), bigb.next()]
            for i in range(2):
                g = blk * 2 + i
                S.op(SP, lambda e, x_=xt[i], g=g: e.dma_start(out=x_[:], in_=XB[:, :, g * 512:(g + 1) * 512].rearrange("c p t -> p c t")), reads=[Dq("XB", g)], writes=[xt[i]], dma=True)
            tA = [load_tabs(blk * 2 + i, [0, 1], i) for i in range(2)]
            pend = []
            for gi in range(8):
                wb = wsm.next()
                wv = wb[:].rearrange("p (k n) -> p k n", k=8)
                load_w(wv, W[:, gi * 256:(gi + 1) * 256].rearrange("(k p) n -> p k n", p=128), wb)
                dst = QC if gi < 4 else KC
                nm = "QC" if gi < 4 else "KC"
                for i in range(2):
                    g = blk * 2 + i
                    tsl = slice(g * 512, (g + 1) * 512)
                    for jj in range(2):
                        j = (gi % 4) * 2 + jj
                        P1 = ps_all.next()
                        mm_group(P1, 128, 512, [(wv[:, k, jj * 128:jj * 128 + 128], xt[i][:, k, :]) for k in range(8)], [wb, xt[i]])
                        if pend:
                            rope_store_r(*pend.pop())
                        pend.append((P1, tA[i][0], tA[i][1], dst[j, :, tsl], Dq(nm, g)))
            if pend:
                rope_store_r(*pend.pop())
            for half in range(2):
                wb = wring.next()
                wv = wb[:, 0:4096].rearrange("p (k n) -> p k n", k=8)
                load_w(wv, Wd_["w1v"][:, half * 512:(half + 1) * 512].rearrange("(k p) n -> p k n", p=128), wb)
                for i in range(2):
                    g = blk * 2 + i
                    for j in range(4):
                        P = ps_all.next()
                        mm_group(P, 128, 512, [(xt[i][:, k, j * 128:(j + 1) * 128], wv[:, k, :]) for k in range(8)], [wb, xt[i]])
                        o = b16t.next()
                        S.op(ACT, lambda e, o=o, P=P: e.copy(out=o[:], in_=P[:]), reads=[P], writes=[o])
                        S.op(SP, lambda e, o=o, j=j, g=g, half=half: e.dma_start(out=VC[g * 512 + j * 128:g * 512 + (j + 1) * 128, half * 512:(half + 1) * 512], in_=o[:]), reads=[o], accum=[Dq("VC", g)], dma=True)

    psS = Ring(PSB[0:4])
    psO = Ring(PSB[4:6])
    att_slot = [0]

    def attention(kind):
        mk = mask_np()

        if kind == "A":
            for c0 in range(0, MASK_W, 496):
                S.op(POOL, lambda e, c0=c0: e.dma_start(out=wres[:, c0:c0 + 496], in_=Wd_["maskA"][:, c0:c0 + 496]), writes=([wres] if c0 == 0 else []), accum=([] if c0 == 0 else [wres]), dma=True)

        def load_head(s, h):
            seqbufs = [g for g in range(s * 4, s * 4 + 4)]
            sl = att_slot[0]
            att_slot[0] = (sl + 1) % 2
            kT = harena[:, (sl * 3 + 0) * 2048:(sl * 3 + 1) * 2048]
            qT = harena[:, (sl * 3 + 1) * 2048:(sl * 3 + 2) * 2048]
            Vt = harena[:, (sl * 3 + 2) * 2048:(sl * 3 + 3) * 2048].rearrange("p (c d) -> p c d", c=16)
            kb, qb, vb = att[sl * 3], att[sl * 3 + 1], att[sl * 3 + 2]
            tok = slice(s * T, (s + 1) * T)
            if kind == "A":
                dq, scale = 64, 64 ** -0.5
                p0 = (h % 2) * 64
                ksrc, qsrc = KA[h // 2, p0:p0 + 64, tok], QA[h // 2, p0:p0 + 64, tok]
                vsrc = VA[tok, h * 64:(h + 1) * 64].rearrange("(c p) d -> p c d", p=128)
                names = ("KA", "QA", "VA")
                dv = 64
            elif kind == "B":
                dq, scale = 96, 96 ** -0.5
                ksrc, qsrc = KB[h, :, tok], QB[h, :, tok]
                vsrc = VB[tok, h * 64:(h + 1) * 64].rearrange("(c p) d -> p c d", p=128)
                names = ("KB", "QB", "VB")
                dv = 64
            else:
                dq, scale = 128, 64 ** -0.5
                ksrc, qsrc = KC[h, :, tok], QC[h, :, tok]
                vsrc = VC[tok, h * 128:(h + 1) * 128].rearrange("(c p) d -> p c d", p=128)
                names = ("KC", "QC", "VC")
                dv = 128
            S.op(SP, lambda e: e.dma_start(out=kT[0:dq, :], in_=ksrc), reads=[Dq(names[0], g) for g in seqbufs], writes=[kb], dma=True)
            S.op(SP, lambda e: e.dma_start(out=qT[0:dq, :], in_=qsrc), reads=[Dq(names[1], g) for g in seqbufs], writes=[qb], dma=True)
            S.op(SP, lambda e: e.dma_start(out=Vt[:, :, 0:dv], in_=vsrc), reads=[Dq(names[2], g) for g in seqbufs], writes=[vb], dma=True)
            if dv == 64:
                S.op(POOL, lambda e: e.memset(Vt[:, :, 64:128], 1.0), accum=[vb])
            return (kT, qT, Vt, kb, qb, vb, dq, scale, dv)

        def do_head(s, h, ctx):
            kT, qT, Vt, kb, qb, vb, dq, scale, dv = ctx

            def do_qt_ab(qt):
                qsl = slice(qt * 512, (qt + 1) * 512)
                g = s * 4 + qt
                if kind == "A":
                    kcs = [kc for kc in range(16) if mk[:, qt * 512 - kc * 128 + MASK_U0: qt * 512 - kc * 128 + MASK_U0 + 512].any()]
                else:
                    kcs = list(range(16))
                n = len(kcs)
                O = psO.next()
                Eb = {}

                def issue_s(i):
                    kc = kcs[i]
                    P = psS.next()
                    mm_group(P, 128, 512, [(kT[0:dq, kc * 128:(kc + 1) * 128], qT[0:dq, qsl])], [kb, qb])
                    E = b16t.next()
                    S.op(ACT, lambda e: e.activation(out=E[:], in_=P[:], func=AF.Exp, scale=float(scale)), reads=[P], writes=[E])
                    if kind == "A":
                        base = qt * 512 - kc * 128 + MASK_U0
                        S.op(DVE, lambda e: e.tensor_tensor(out=E[:], in0=E[:], in1=wres[:, base:base + 512], op=ALU.mult), reads=[E, wres], writes=[E])
                    Eb[i] = E

                def issue_pv(i):
                    kc = kcs[i]
                    E = Eb.pop(i)
                    S.op(PE, lambda e: e.matmul(O[:], lhsT=Vt[:, kc, :], rhs=E[:], start=(i == 0), stop=(i == n - 1)), reads=[vb, E], writes=[O])
                LOOK = 3
                for i in range(min(LOOK, n)):
                    issue_s(i)
                for i in range(n):
                    if i + LOOK < n:
                        issue_s(i + LOOK)
                    issue_pv(i)
                R = f32t.next(); o = b16t.next()
                S.op(ACT, lambda e: e.activation(out=R[0:64, :], in_=O[64:128, :], func=AF.Ln), reads=[O], writes=[R])
                S.op(ACT, lambda e: e.activation(out=R[0:64, :], in_=R[0:64, :], func=AF.Exp, scale=-1.0), reads=[R], writes=[R])
                S.op(DVE, lambda e: e.tensor_tensor(out=o[0:64, :], in0=O[0:64, :], in1=R[0:64, :], op=ALU.mult), reads=[O, R], writes=[o])
                mixrow = (0 if kind == "A" else 512) + h * 64
                c_, r_ = mixrow // 128, mixrow % 128
                S.op(SP, lambda e: e.dma_start(out=AO[c_, r_:r_ + 64, g * 512:(g + 1) * 512], in_=o[0:64, :]), reads=[o], accum=[Dq("AO", g)], dma=True)

            def do_qt_c(qt):
                qsl = slice(qt * 512, (qt + 1) * 512)
                g = s * 4 + qt
                n = 16
                U0, U1, L0, L1 = PSB[4], PSB[5], PSB[6], PSB[7]
                Eb = {}

                def issue_s(i):
                    Es = []
                    for c in range(2):
                        P = psS.next()
                        mm_group(P, 128, 512, [(kT[c * 64:(c + 1) * 64, i * 128:(i + 1) * 128], qT[c * 64:(c + 1) * 64, qsl])], [kb, qb])
                        E = b16t.next()
                        S.op(ACT, lambda e, E=E, P=P: e.activation(out=E[:], in_=P[:], func=AF.Exp, scale=float(scale)), reads=[P], writes=[E])
                        Es.append(E)
                    Eb[i] = Es

                def issue_pv(i):
                    Es = Eb.pop(i)
                    for c, (U, L) in enumerate(((U0, L0), (U1, L1))):
                        E = Es[c]
                        S.op(PE, lambda e, U=U, E=E: e.matmul(U[:], lhsT=Vt[:, i, :], rhs=E[:], start=(i == 0), stop=(i == n - 1)), reads=[vb, E], writes=[U])
                        S.op(PE, lambda e, L=L, E=E: e.matmul(L[:], lhsT=ones[:], rhs=E[:], start=(i == 0), stop=(i == n - 1)), reads=[ones, E], writes=[L])
                issue_s(0)
                for i in range(n):
                    if i + 1 < n:
                        issue_s(i + 1)
                    issue_pv(i)
                r0 = f32t.next(); r1 = f32t.next(); t0 = f32t.next(); t1 = f32t.next(); osq = b16t.next(); o = b16t.next()
                S.op(ACT, lambda e: e.activation(out=r0[:], in_=L0[:], func=AF.Ln), reads=[L0], writes=[r0])
                S.op(ACT, lambda e: e.activation(out=r1[:], in_=L1[:], func=AF.Ln), reads=[L1], writes=[r1])
                S.op(DVE, lambda e: e.tensor_copy(out=t0[:], in_=U0[:]), reads=[U0], writes=[t0])
                S.op(DVE, lambda e: e.tensor_copy(out=t1[:], in_=U1[:]), reads=[U1], writes=[t1])
                S.op(ACT, lambda e: e.activation(out=r0[:], in_=r0[:], func=AF.Exp, scale=-1.0), reads=[r0], writes=[r0])
                S.op(ACT, lambda e: e.activation(out=r1[:], in_=r1[:], func=AF.Exp, scale=-1.0), reads=[r1], writes=[r1])
                S.op(DVE, lambda e: e.tensor_tensor(out=t0[:], in0=t0[:], in1=r0[:], op=ALU.mult), reads=[t0, r0], writes=[t0])
                S.op(DVE, lambda e: e.tensor_tensor(out=t1[:], in0=t1[:], in1=r1[:], op=ALU.mult), reads=[t1, r1], writes=[t1])
                S.op(DVE, lambda e: e.scalar_tensor_tensor(out=t0[:], in0=t1[:], scalar=nlam, in1=t0[:], op0=ALU.mult, op1=ALU.add), reads=[t0, t1, lamt], writes=[t0])
                S.op(ACT, lambda e: e.activation(out=osq[:], in_=t0[:], func=AF.Square), reads=[t0], writes=[osq])
                Pss = psS.next()
                mm_group(Pss, 128, 512, [(ones[:], osq[:])], [ones, osq])
                r = rstd_from_ss(Pss, 1.0 / 128, 5)
                S.op(DVE, lambda e: e.scalar_tensor_tensor(out=o[:], in0=t0[:], scalar=gsub[:, 0:1], in1=r[:], op0=ALU.mult, op1=ALU.mult), reads=[t0, r, gsub], writes=[o])
                S.op(SP, lambda e: e.dma_start(out=AO[h, :, g * 512:(g + 1) * 512], in_=o[:]), reads=[o], accum=[Dq("AO", g)], dma=True)

            for qt in range(4):
                if kind == "C":
                    do_qt_c(qt)
                else:
                    do_qt_ab(qt)

        order = [(s, h) for s in range(NSEQ) for h in range(8)]
        ctxs = {0: load_head(*order[0])}
        for n_, (s, h) in enumerate(order):
            if n_ + 1 < len(order):
                ctxs[n_ + 1] = load_head(*order[n_ + 1])
            do_head(s, h, ctxs.pop(n_))

    def layer_norm_gen(xf8, X, l, goff, boff, outb, outbuf, res):
        zb = bigb.next(); zq = bigb.next()
        res["zq"] = zq
        for c in range(8):
            S.op(ACT, lambda e, c=c: e.copy(out=zb[:, c, :], in_=xf8[:, c, :]), reads=[X.ch[c]], accum=[zb])
            S.op(ACT, lambda e, c=c: e.activation(out=zq[:, c, :], in_=xf8[:, c, :], func=AF.Square), reads=[X.ch[c]], accum=[zq])
        yield
        S1 = ps_all.next(); S2 = ps_all.next()
        mm_group(S1, 128, 512, [(ones[:], zb[:, c, :]) for c in range(8)], [ones, zb])
        mm_group(S2, 128, 512, [(ones[:], zq[:, c, :]) for c in range(8)], [ones, zq])
        m, msq, var, nmr = tabs[0], tabs[1], tabs[2], tabs[3]
        S.op(ACT, lambda e: e.mul(out=m[:], in_=S1[:], mul=1.0 / D), reads=[S1], writes=[m])
        S.op(DVE, lambda e: e.tensor_tensor(out=msq[:], in0=m[:], in1=m[:], op=ALU.mult), reads=[m], writes=[msq])
        S.op(DVE, lambda e: e.scalar_tensor_tensor(out=var[:], in0=S2[:], scalar=1.0 / D, in1=msq[:], op0=ALU.mult, op1=ALU.subtract), reads=[S2, msq], writes=[var])
        S.op(ACT, lambda e: e.activation(out=var[:], in_=var[:], func=AF.Ln, bias=rc[:, 5:6], scale=1.0), reads=[var, rc], writes=[var])
        S.op(ACT, lambda e: e.activation(out=var[:], in_=var[:], func=AF.Exp, scale=-0.5), reads=[var], writes=[var])
        S.op(DVE, lambda e: e.scalar_tensor_tensor(out=nmr[:], in0=m[:], scalar=-1.0, in1=var[:], op0=ALU.mult, op1=ALU.mult), reads=[m, var], writes=[nmr])
        yield
        for c in range(8):
            t = f32t.next()
            S.op(DVE, lambda e, c=c, t=t: e.scalar_tensor_tensor(out=t[:], in0=xf8[:, c, :], scalar=V_(l, goff, c), in1=var[:], op0=ALU.mult, op1=ALU.mult), reads=[X.ch[c], var, vec], writes=[t])
            S.op(DVE, lambda e, c=c, t=t: e.scalar_tensor_tensor(out=xf8[:, c, :], in0=nmr[:], scalar=V_(l, goff, c), in1=t[:], op0=ALU.mult, op1=ALU.add), reads=[nmr, vec, t], writes=[X.ch[c]])
            S.op(ACT, lambda e, c=c: e.activation(out=outb[:, c, :], in_=xf8[:, c, :], func=AF.Identity, bias=V_(l, boff, c), scale=1.0), reads=[X.ch[c], vec], accum=[outbuf])
            S.op(ACT, lambda e, c=c: e.activation(out=xf8[:, c, :], in_=xf8[:, c, :], func=AF.Identity, bias=V_(l, boff, c), scale=1.0), reads=[X.ch[c], vec], writes=[X.ch[c]])

    def layer_norm(xf8, X, l, goff, boff, outb, outbuf):
        res = {}
        for _ in layer_norm_gen(xf8, X, l, goff, boff, outb, outbuf, res):
            pass
        return res["zq"]

    def interleave(ga, gb):
        live = [gb, ga]
        while live:
            for g_ in list(live):
                try:
                    next(g_)
                except StopIteration:
                    live.remove(g_)

    def phase3a(l):
        wo = wres[:].rearrange("p (k n) -> p k n", k=8)
        load_w(wo, Wd_["wo%d" % l].rearrange("(k p) n -> p k n", p=128), wres)

        def tile3a_A(g, st):
            gsl = slice(g * 512, (g + 1) * 512)
            B = bigs.next()
            xf8 = xf_view(B)
            st[g] = (B, xf8)
            ao = bigb.next()
            S.op(SP, lambda e: e.dma_start(out=ao[:], in_=AO[:, :, gsl].rearrange("c p t -> p c t")), reads=[Dq("AO", g)], writes=[ao], dma=True)
            S.op(SP, lambda e: e.dma_start(out=xf8, in_=XF[:, :, gsl].rearrange("c p t -> p c t")), reads=[Dq("XF", g)], writes=B.ch, dma=True)
            for c in range(8):
                P = ps_all.next()
                mm_group(P, 128, 512, [(wo[:, k, c * 128:(c + 1) * 128], ao[:, k, :]) for k in range(8)], [wres, ao])
                S.op(DVE, lambda e, c=c, P=P: e.scalar_tensor_tensor(out=xf8[:, c, :], in0=xf8[:, c, :], scalar=ALPHA, in1=P[:], op0=ALU.mult, op1=ALU.add), reads=[B.ch[c], P], writes=[B.ch[c]])
                if c == 3:
                    yield

        def tile3a_B(g, B, xf8):
            gsl = slice(g * 512, (g + 1) * 512)
            ob = bigb.next()
            yield from layer_norm_gen(xf8, B, l, 0, 8, ob, ob, {})
            S.op(ACT, lambda e: e.dma_start(out=XF[:, :, gsl].rearrange("c p t -> p c t"), in_=xf8), reads=B.ch, writes=[Dq("XF", g)], dma=True)
            S.op(ACT, lambda e: e.dma_start(out=X1B[:, :, gsl].rearrange("c p t -> p c t"), in_=ob[:]), reads=[ob], writes=[Dq("X1B", g)], dma=True)

        st = {}
        for _ in tile3a_A(0, st):
            pass
        for g in range(8):
            ga = tile3a_A(g + 1, st) if g + 1 < 8 else iter(())
            interleave(ga, tile3a_B(g, *st[g]))

    hT = harena[:].rearrange("p (f t) -> p f t", f=NFC)

    def phase3b(l, last):
        wpg = wres[:].rearrange("p (k n) -> p k n", k=8)
        load_w(wpg, Wd_["wpg%d" % l].rearrange("(k p) n -> p k n", p=128), wres)
        load_w(wpp_t[:], Wd_["wpp%d" % l].rearrange("(k p) n -> p k n", p=128), wpp_t)
        WG, WU, WD = Wd_["wg%d" % l], Wd_["wu%d" % l], Wd_["wd%d" % l]

        def ffn_chunk(i, fc, f2, wg, wgb, wu, wub, xbh, xbh_v):
            cc = f32t.next()
            for sub in range(2):
                t0 = i * 512 + sub * 256
                G = ps_all.next()
                mm_group(G, 128, 258, [(wg[:, k, f2 * 128:(f2 + 1) * 128], xbh_v[:, k, t0:t0 + 258]) for k in range(8)], [wgb] + xbh.ch)
                csl = slice(sub * 256, (sub + 1) * 256)
                S.op(ACT, lambda e, G=G, csl=csl: e.activation(out=cc[:, csl], in_=G[:, 1:257], func=AF.Identity, scale=V_(l, 32 + 22, fc), bias=V_(l, 32 + 66, fc)), reads=[G, vec], accum=[cc])
                S.op(DVE, lambda e, G=G, csl=csl: e.scalar_tensor_tensor(out=cc[:, csl], in0=G[:, 0:256], scalar=V_(l, 32, fc), in1=cc[:, csl], op0=ALU.mult, op1=ALU.add), reads=[G, vec, cc], accum=[cc])
                S.op(DVE, lambda e, G=G, csl=csl: e.scalar_tensor_tensor(out=cc[:, csl], in0=G[:, 2:258], scalar=V_(l, 32 + 44, fc), in1=cc[:, csl], op0=ALU.mult, op1=ALU.add), reads=[G, vec, cc], accum=[cc])
            U = ps_all.next()
            mm_group(U, 128, 512, [(wu[:, k, f2 * 128:(f2 + 1) * 128], xbh_v[:, k, 1 + i * 512:1 + (i + 1) * 512]) for k in range(8)], [wub] + xbh.ch)
            gg = f32t.next()
            S.op(ACT, lambda e: e.activation(out=gg[:], in_=cc[:], func=AF.Gelu_apprx_tanh), reads=[cc], writes=[gg])
            S.op(DVE, lambda e: e.tensor_tensor(out=hT[:, fc, i * 512:(i + 1) * 512], in0=gg[:], in1=U[:], op=ALU.mult), reads=[gg, U], accum=[harena] + h_alias)

        def post_A2(blk, st):
            for i in range(2):
                g = blk * 2 + i
                gsl = slice(g * 512, (g + 1) * 512)
                B = bigs.next()
                xf8 = xf_view(B)
                st[i] = (B, xf8)
                S.op(SP, lambda e, xf8=xf8, gsl=gsl: e.dma_start(out=xf8, in_=XF[:, :, gsl].rearrange("c p t -> p c t")), reads=[Dq("XF", g)], writes=B.ch, dma=True)
            for dg in range(4):
                wdb = wring.next()
                wd = wdb[:, 0:5632].rearrange("p (f n) -> p f n", f=NFC)
                load_w(wd, WD[:, dg * 256:(dg + 1) * 256].rearrange("(f p) n -> p f n", p=128), wdb)
                for i in range(2):
                    B, xf8 = st[i]
                    for d2 in range(2):
                        c = dg * 2 + d2
                        P = ps_all.next()
                        mm_group(P, 128, 512, [(wd[:, f, d2 * 128:(d2 + 1) * 128], hT[:, f, i * 512:(i + 1) * 512]) for f in range(NFC)], [wdb, harena] + h_alias)
                        S.op(DVE, lambda e, c=c, P=P, xf8=xf8: e.scalar_tensor_tensor(out=xf8[:, c, :], in0=xf8[:, c, :], scalar=ALPHA, in1=P[:], op0=ALU.mult, op1=ALU.add), reads=[B.ch[c], P], writes=[B.ch[c]])

        def post_B(blk, i, big32, xf8):
            g = blk * 2 + i
            gsl = slice(g * 512, (g + 1) * 512)
            S.op(SP, lambda e: e.dma_start(out=ptile[:], in_=PT[l, :, :, gsl].rearrange("k p t -> p k t")), reads=[Dq("PT%d" % l, g)], writes=[ptile], dma=True)
            x2b = bigb.next()
            res = {}
            yield from layer_norm_gen(xf8, big32, l, 16, 24, x2b, x2b, res)
            x3b = res["zq"]
            for c in range(8):
                Pg = ps_all.next(); Pp = ps_all.next()
                mm_group(Pg, 128, 512, [(wpg[:, k, c * 128:(c + 1) * 128], x2b[:, k, :]) for k in range(8)], [wres, x2b])
                mm_group(Pp, 128, 512, [(wpp_t[:, k, c * 128:(c + 1) * 128], ptile[:, k, :]) for k in range(2)], [wpp_t, ptile])
                sg = f32t.next(); tt = f32t.next()
                S.op(ACT, lambda e, sg=sg, Pg=Pg: e.activation(out=sg[:], in_=Pg[:], func=AF.Sigmoid), reads=[Pg], writes=[sg])
                S.op(DVE, lambda e, sg=sg, tt=tt, Pp=Pp: e.tensor_tensor(out=tt[:], in0=sg[:], in1=Pp[:], op=ALU.mult), reads=[sg, Pp], writes=[tt])
                S.op(DVE, lambda e, c=c, tt=tt: e.tensor_tensor(out=xf8[:, c, :], in0=xf8[:, c, :], in1=tt[:], op=ALU.add), reads=[big32.ch[c], tt], writes=[big32.ch[c]])
                if not last:
                    S.op(ACT, lambda e, c=c: e.copy(out=x3b[:, c, :], in_=xf8[:, c, :]), reads=[big32.ch[c]], accum=[x3b])
            if not last:
                S.op(ACT, lambda e: e.dma_start(out=XF[:, :, gsl].rearrange("c p t -> p c t"), in_=xf8), reads=big32.ch, writes=[Dq("XF", g)], dma=True)
                S.op(ACT, lambda e: e.dma_start(out=XB[:, :, gsl].rearrange("c p t -> p c t"), in_=x3b[:]), reads=[x3b], writes=[Dq("XB", g)], dma=True)
            else:
                for j in range(4):
                    for hh in range(2):
                        P = ps_all.next()

                        def tro(e, P=P, j=j, hh=hh):
                            for c4 in range(4):
                                c = hh * 4 + c4
                                ins = e.transpose(out=P[:, c4 * 128:(c4 + 1) * 128], in_=xf8[:, c, j * 128:(j + 1) * 128], identity=ident[:])
                            return ins
                        S.op(PE, tro, reads=big32.ch + [ident], writes=[P])
                        f = f32t.next()
                        S.op(ACT, lambda e, f=f, P=P: e.copy(out=f[:], in_=P[:]), reads=[P], writes=[f])
                        r0 = g * 512 + j * 128
                        ev = S.op(SP, lambda e, f=f, r0=r0, hh=hh: e.dma_start(out=out_d[r0:r0 + 128, hh * 512:(hh + 1) * 512], in_=f[:]), reads=[f], dma=True)
                        S.out_evs.append(ev)

        def blk3b(blk):
            xbh = bigs.next()
            xbh_v = xbh_view(xbh)
            b0 = blk * 1024
            first = (blk % 2 == 0)
            lastb = (blk % 2 == 1)
            lo = b0 - (0 if first else 1)
            hi = b0 + 1024 + (0 if lastb else 1)
            dcol = 1 if first else 0
            rd = [Dq("X1B", g) for g in range(max(0, blk * 2 - 1), min(8, blk * 2 + 3))]
            S.op(SP, lambda e: e.dma_start(out=xbh_v[:, :, dcol:dcol + (hi - lo)], in_=X1B[:, :, lo:hi].rearrange("c p t -> p c t")), reads=rd, writes=xbh.ch, dma=True)
            if first:
                S.op(DVE, lambda e: e.memset(xbh_v[:, :, 0:1], 0.0), accum=xbh.ch)
            if lastb:
                S.op(DVE, lambda e: e.memset(xbh_v[:, :, 1025:1026], 0.0), accum=xbh.ch)
            for fg in range(11):
                wgb = wsm.next(); wub = wsm.next()
                wg = wgb[:].rearrange("p (k n) -> p k n", k=8)
                wu = wub[:].rearrange("p (k n) -> p k n", k=8)
                load_w(wg, WG[:, fg * 256:(fg + 1) * 256].rearrange("(k p) n -> p k n", p=128), wgb)
                load_w(wu, WU[:, fg * 256:(fg + 1) * 256].rearrange("(k p) n -> p k n", p=128), wub)
                for i in range(2):
                    for f2 in range(2):
                        ffn_chunk(i, fg * 2 + f2, f2, wg, wgb, wu, wub, xbh, xbh_v)
            st = {}
            post_A2(blk, st)
            for i in range(2):
                for _ in post_B(blk, i, *st[i]):
                    pass

        for blk in range(4):
            blk3b(blk)

    stages = [("p1_0", phase1_layer0), ("attA", lambda: attention("A")), ("attB", lambda: attention("B")),
              ("p3a_0", lambda: phase3a(0)), ("p3b_0", lambda: phase3b(0, False)),
              ("p1_1", phase1_layer1), ("attC", lambda: attention("C")),
              ("p3a_1", lambda: phase3a(1)), ("p3b_1", lambda: phase3b(1, True))]
    for nm, fn in stages:
        if stop_after == "none":
            break
        fn()
        if stop_after == nm:
            break
    if not S.out_evs:
        for e_ in ("pe", "act", "dve", "pool"):
            if S.cnt[e_]:
                S.out_evs.append(("e" + e_, S.cnt[e_]))
        for k, v in S.dma_cnt.items():
            S.out_evs.append((k, v))
    S.emit(es)
    es.close()
    return nc


_CONSTS = None


def kernel(**inputs):
    global _CONSTS
    inp = {k: np.asarray(v) for k, v in inputs.items()}
    w = pack_weights(inp)
    if _CONSTS is None:
        _CONSTS = make_consts()
    w.update(_CONSTS)
    x = np.asarray(inp["x"], np.float32)
    p = np.asarray(inp["p"], np.float32)
    pos = np.asarray(inp["positions"], np.int32)
    nc = build_program()
    in_maps = []
    for c in range(8):
        m = dict(w)
        m["x"] = np.ascontiguousarray(x[2 * c:2 * c + 2].reshape(NT, D))
        m["p"] = np.ascontiguousarray(p[:, 2 * c:2 * c + 2].reshape(DEPTH, NT, 256))
        m["pos"] = np.ascontiguousarray(pos[2 * c:2 * c + 2])
        in_maps.append(m)
    res = run_bass_kernel_spmd(nc, in_maps, core_ids=list(range(8)))
    out = np.concatenate([np.asarray(r["out"], np.float32).reshape(2, T, D) for r in res.results], axis=0)
    return out
```
